# Optimizing a Trainium2 kernel written in Bass

```python
import jax
import jax.numpy as jnp
from jax import lax
import numpy as np

D_MODEL = 1024
BATCH = 2
SEQ = 8192
DEPTH = 1

CHUNK = 64
LEFT_CHUNKS = 8
N_HEADS = 8
HEAD_DIM = 64
ATTN_WIDTH = N_HEADS * HEAD_DIM
POOL_WINDOWS = (2, 4, 8, 16)
POOL_GROUPS = len(POOL_WINDOWS)
POOL_WIDTH = 512
POOL_GROUP_DIM = POOL_WIDTH // POOL_GROUPS
REL_CLIP = 128
D_FF = 2816
N_BRANCHES = 2
IN_COLS = 3 * ATTN_WIDTH + POOL_WIDTH + N_BRANCHES * D_MODEL
N_ADA = 9
EPS = 1e-6

kernel_name = 'hybrid_chunk_attn_pool_block'


def rms_norm(x, g):
    xf = x.astype(jnp.float32)
    y = xf * lax.rsqrt(jnp.mean(xf * xf, axis=-1, keepdims=True) + EPS)
    return (y * g.astype(jnp.float32)).astype(x.dtype)


def modulate(h, shift, scale):
    return h * (1 + scale) + shift


def swiglu(h, w_in, w_out):
    a, b = jnp.split(h @ w_in, 2, axis=-1)
    return (jax.nn.silu(a) * b) @ w_out


def chunk_band_attention(q, k, v, q_gain, k_gain, rel_bias):
    b, s, _ = q.shape
    nc = s // CHUNK
    band = (LEFT_CHUNKS + 1) * CHUNK
    q = rms_norm(q.reshape(b, nc, CHUNK, N_HEADS, HEAD_DIM), q_gain)
    k = rms_norm(k.reshape(b, nc, CHUNK, N_HEADS, HEAD_DIM), k_gain)
    v = v.reshape(b, nc, CHUNK, N_HEADS, HEAD_DIM)
    pad = ((0, 0), (LEFT_CHUNKS, 0), (0, 0), (0, 0), (0, 0))
    kp = jnp.pad(k, pad)
    vp = jnp.pad(v, pad)
    kb = jnp.concatenate([kp[:, w:w + nc] for w in range(LEFT_CHUNKS + 1)], axis=2)
    vb = jnp.concatenate([vp[:, w:w + nc] for w in range(LEFT_CHUNKS + 1)], axis=2)
    scores = jnp.einsum('bnqhd,bnkhd->bnhqk', q, kb).astype(jnp.float32) * (HEAD_DIM ** -0.5)
    r = np.arange(CHUNK)
    j = np.arange(band)
    dist = (LEFT_CHUNKS - j // CHUNK)[None, :] * CHUNK + r[:, None] - (j % CHUNK)[None, :]
    idx = np.clip(dist, -REL_CLIP, REL_CLIP) + REL_CLIP
    bias = rel_bias[:, idx].astype(jnp.float32)
    key_chunk = np.arange(nc)[:, None] - LEFT_CHUNKS + (j // CHUNK)[None, :]
    valid = jnp.asarray(key_chunk >= 0)[None, :, None, None, :]
    scores = jnp.where(valid, scores + bias[None, None], -1e30)
    p = jax.nn.softmax(scores, axis=-1).astype(v.dtype)
    out = jnp.einsum('bnhqk,bnkhd->bnqhd', p, vb)
    return out.reshape(b, s, ATTN_WIDTH)


def multiscale_pool(u, w_group, scale):
    b, s, _ = u.shape
    ug = u.reshape(b, s, POOL_GROUPS, POOL_GROUP_DIM).astype(jnp.float32)
    cs = jnp.pad(jnp.cumsum(ug, axis=1), ((0, 0), (1, 0), (0, 0), (0, 0)))
    t = np.arange(s)[:, None]
    win = np.array(POOL_WINDOWS)[None, :]
    start = np.maximum(t + 1 - win, 0)
    count = (t + 1 - start).astype(np.float32)
    lo = cs[:, start, np.arange(POOL_GROUPS)[None, :], :]
    mean = (cs[:, 1:] - lo) / count[None, :, :, None]
    mixed = (mean - ug).astype(u.dtype)
    y = jnp.einsum('bsgc,gcd->bsgd', mixed, w_group).reshape(b, s, POOL_WIDTH)
    return y * scale


def token_mix(h, w_in, q_gain, k_gain, rel_bias, w_attn_out, w_pool_group, pool_scale, w_pool_out, w_o):
    z = h @ w_in
    a3 = 3 * ATTN_WIDTH
    q, k, v, u, ga, gb = jnp.split(
        z, [ATTN_WIDTH, 2 * ATTN_WIDTH, a3, a3 + POOL_WIDTH, a3 + POOL_WIDTH + D_MODEL], axis=-1)
    ya = chunk_band_attention(q, k, v, q_gain, k_gain, rel_bias) @ w_attn_out
    yb = multiscale_pool(u, w_pool_group, pool_scale) @ w_pool_out
    merged = jax.nn.sigmoid(ga) * ya + jax.nn.sigmoid(gb) * yb
    return merged @ w_o


def setup_inputs(seed: int = 0) -> dict:
    key = jax.random.key(seed)
    ks = jax.random.split(key, 24)
    f32 = jnp.float32
    L = DEPTH

    def nrm(k, shape, scale):
        return jax.random.normal(k, shape, f32) * scale

    def gain(k, shape):
        return 1.0 + 0.05 * jax.random.normal(k, shape, f32)

    return {
        'x': nrm(ks[0], (BATCH, SEQ, D_MODEL), 1.0),
        'c': nrm(ks[1], (BATCH, D_MODEL), 1.0),
        'w_ada': nrm(ks[2], (L, D_MODEL, N_ADA * D_MODEL), D_MODEL ** -0.5),
        'b_ada': nrm(ks[3], (L, N_ADA * D_MODEL), 0.02),
        'g_ffn1': gain(ks[4], (L, D_MODEL)),
        'w_ffn1_in': nrm(ks[5], (L, D_MODEL, 2 * D_FF), D_MODEL ** -0.5),
        'w_ffn1_out': nrm(ks[6], (L, D_FF, D_MODEL), D_FF ** -0.5),
        'g_mix': gain(ks[7], (L, D_MODEL)),
        'w_in': nrm(ks[8], (L, D_MODEL, IN_COLS), D_MODEL ** -0.5),
        'q_gain': gain(ks[9], (L, HEAD_DIM)),
        'k_gain': gain(ks[10], (L, HEAD_DIM)),
        'rel_bias': nrm(ks[11], (L, N_HEADS, 2 * REL_CLIP + 1), 0.5),
        'w_attn_out': nrm(ks[12], (L, ATTN_WIDTH, D_MODEL), ATTN_WIDTH ** -0.5),
        'w_pool_group': nrm(ks[13], (L, POOL_GROUPS, POOL_GROUP_DIM, POOL_GROUP_DIM), POOL_GROUP_DIM ** -0.5),
        'pool_scale': gain(ks[14], (L, POOL_WIDTH)),
        'w_pool_out': nrm(ks[15], (L, POOL_WIDTH, D_MODEL), POOL_WIDTH ** -0.5),
        'w_o': nrm(ks[16], (L, D_MODEL, D_MODEL), D_MODEL ** -0.5),
        'g_ffn2': gain(ks[17], (L, D_MODEL)),
        'w_ffn2_in': nrm(ks[18], (L, D_MODEL, 2 * D_FF), D_MODEL ** -0.5),
        'w_ffn2_out': nrm(ks[19], (L, D_FF, D_MODEL), D_FF ** -0.5),
    }


def reference(x, c, w_ada, b_ada, g_ffn1, w_ffn1_in, w_ffn1_out, g_mix, w_in, q_gain, k_gain,
              rel_bias, w_attn_out, w_pool_group, pool_scale, w_pool_out, w_o, g_ffn2,
              w_ffn2_in, w_ffn2_out):
    b = x.shape[0]
    cc = jax.nn.silu(c)
    for l in range(DEPTH):
        mod = (cc @ w_ada[l] + b_ada[l]).reshape(b, N_ADA, D_MODEL)
        sh1, sc1, gt1, sh2, sc2, gt2, sh3, sc3, gt3 = [mod[:, i, None, :] for i in range(N_ADA)]
        h = modulate(rms_norm(x, g_ffn1[l]), sh1, sc1)
        x = x + 0.5 * gt1 * swiglu(h, w_ffn1_in[l], w_ffn1_out[l])
        h = modulate(rms_norm(x, g_mix[l]), sh2, sc2)
        x = x + gt2 * token_mix(h, w_in[l], q_gain[l], k_gain[l], rel_bias[l], w_attn_out[l],
                                w_pool_group[l], pool_scale[l], w_pool_out[l], w_o[l])
        h = modulate(rms_norm(x, g_ffn2[l]), sh3, sc3)
        x = x + 0.5 * gt3 * swiglu(h, w_ffn2_in[l], w_ffn2_out[l])
    return x
```

```python
import numpy as np
import concourse.bass as bass
import concourse.mybir as mybir
from concourse.bass_utils import run_bass_kernel_spmd

F32 = mybir.dt.float32
BF16 = mybir.dt.bfloat16
AF = mybir.ActivationFunctionType
ALU = mybir.AluOpType

D = 1024
DFF = 2816
NJ = DFF // 128
TM = 2048
TB = 512
EPS = 1e-6
NCST = 176
TABW = 896


class Op:
    __slots__ = ("eng", "fn", "deps", "marked", "sem", "count", "dma")


class Res:
    __slots__ = ("w", "r")

    def __init__(self):
        self.w = None
        self.r = {}


class Sched:
    ENG = ("pe", "act", "dve", "pool", "sp")

    def __init__(self):
        self.q = {e: [] for e in self.ENG}
        self.bar = []
        self.frozen = False

    def op(self, eng, fn, reads=(), writes=(), deps=(), dma=None, nobar=False, force=False):
        o = Op()
        if self.frozen and not force:
            o.eng = eng
            o.dma = dma
            o.marked = False
            o.deps = []
            return o
        o.eng = eng
        o.fn = fn
        o.dma = dma
        o.marked = dma is not None
        o.sem = None
        o.count = 0
        d = []
        for r in reads:
            if r.w is not None:
                d.append(r.w)
        for w in writes:
            if w.w is not None:
                d.append(w.w)
            d.extend(w.r.values())
        d.extend(x for x in deps if x is not None)
        if not nobar and eng != "pe":
            d.extend(self.bar)
        dd = []
        seen = set()
        for x in d:
            if x is o or id(x) in seen:
                continue
            if x.eng == "pe" and eng == "pe" and x.dma is None:
                continue
            seen.add(id(x))
            dd.append(x)
            x.marked = True
        o.deps = dd
        for r in reads:
            r.r[(eng, dma)] = o
        for w in writes:
            w.w = o
            w.r = {}
        self.q[eng].append(o)
        return o

    def barrier(self):
        if self.frozen:
            return
        self.bar = [self.q[e][-1] for e in ("pe", "act", "dve") if self.q[e]]

    def finalize(self):
        keys = set()
        for e in self.ENG:
            n = 0
            cum = {}
            for o in self.q[e]:
                if o.dma is not None:
                    o.sem = o.dma
                    cum[o.dma] = cum.get(o.dma, 0) + 16
                    o.count = cum[o.dma]
                    keys.add(o.dma)
                elif o.marked:
                    n += 1
                    o.sem = "p_" + e
                    o.count = n
                    keys.add(o.sem)
        return sorted(keys)

    def emit(self, ename, eng, sems):
        waited = {}
        for o in self.q[ename]:
            need = {}
            for d in o.deps:
                if need.get(d.sem, 0) < d.count:
                    need[d.sem] = d.count
            for k, c in need.items():
                if waited.get(k, 0) >= c:
                    continue
                eng.wait_ge(sems[k], c)
                waited[k] = c
            ins = o.fn(eng)
            if o.marked:
                ins.then_inc(sems[o.sem], 16 if o.dma is not None else 1)


def build_nc(debug=None, stop=None):
    nc = bass.Bass("TRN2", target_bir_lowering=False)

    def din(name, shape):
        return nc.dram_tensor(name, list(shape), F32, kind="ExternalInput").ap()

    xT = din("xT", [D, TM + TB])
    cst_d = din("cst", [128, NCST])
    w_ada = din("w_ada", [D, 9 * D])
    w1i = din("w1i", [D, 2 * DFF])
    w1o = din("w1o", [DFF, D])
    w3i = din("w3i", [D, 2 * DFF])
    w3o = din("w3o", [DFF, D])
    w_in = din("w_in", [D, 4096])
    tab_d = din("tab", [128, 8 * TABW])
    w_ao = din("w_ao", [512, D])
    w_pg = din("w_pg", [512, 128])
    w_po = din("w_po", [512, D])
    w_o = din("w_o", [D, D])
    outT = nc.dram_tensor("outT", [D, TM], F32, kind="ExternalOutput").ap()
    dbg_d = None
    if debug is not None:
        dbg_d = nc.dram_tensor("dbg", [128, debug[1]], F32, kind="ExternalOutput").ap()

    S = Sched()

    ARENA_BYTES = 207 * 1024
    arena_cm = nc.sbuf_tensor("arena", [128, ARENA_BYTES // 2], BF16)
    psum_cm = nc.psum_tensor("ps", [128, 4096], F32)
    arena = arena_cm.__enter__()
    ps = psum_cm.__enter__()
    off = [0]

    def alloc(nbytes):
        o = off[0]
        off[0] += (nbytes + 31) // 32 * 32
        assert off[0] <= ARENA_BYTES, off[0]
        return o

    def view(o, dt, shape):
        n = int(np.prod(shape))
        if dt == BF16:
            v = arena[:, o // 2: o // 2 + n]
        else:
            v = arena[:, o // 2: o // 2 + 2 * n].bitcast(F32)
        if len(shape) == 2:
            return v.rearrange("p (a b) -> p a b", a=shape[0])
        if len(shape) == 3:
            return v.rearrange("p (a b c) -> p a b c", a=shape[0], b=shape[1])
        return v

    o_xb = alloc(8 * TM * 4)
    XB = view(o_xb, F32, (8, TM))
    o_cst = alloc(NCST * 4)
    CST = view(o_cst, F32, (NCST,))
    o_mod = alloc(72 * 4)
    MOD = view(o_mod, F32, (72,))
    o_der = alloc(9 * 8 * 4)
    DER = view(o_der, F32, (9, 8))
    o_scb = alloc(8 * 2)
    SCB = view(o_scb, BF16, (8,))
    o_ones = alloc(128 * 2)
    ONES = view(o_ones, BF16, (128,))
    o_bones = alloc(128 * 2)
    BONES = view(o_bones, BF16, (128,))
    o_vones = alloc(2 * 64 * 2)
    VONES = view(o_vones, BF16, (2, 64))
    o_epsc = alloc(32)
    EPSC = view(o_epsc, F32, (8,))
    o_wpg = alloc(4 * 128 * 2)
    WPG = view(o_wpg, BF16, (4, 128))
    o_w = alloc(10 * 4096)
    WP = [Res() for _ in range(10)]

    def wpiece(i, shape, npieces=1):
        return view(o_w + 4096 * i, BF16, shape)

    o_uni = alloc(0)
    uni_base = off[0]

    off[0] = uni_base
    o_halo = alloc(8 * TB * 4)
    HALO = view(o_halo, F32, (8, TB))
    o_h = alloc(8 * (TM + TB) * 2)
    H = view(o_h, BF16, (8, TM + TB))
    o_actb = alloc(4 * (TM + TB) * 2)
    ACTB = view(o_actb, BF16, (4, TM + TB))
    RSTD = [view(alloc(TB * 4), F32, (TB,)) for _ in range(2)]
    o_tmp = [alloc(TB * 4) for _ in range(2)]
    TMP = [view(o, F32, (TB,)) for o in o_tmp]
    o_sa = [alloc(TB * 2) for _ in range(2)]
    SA = [view(o, BF16, (TB,)) for o in o_sa]
    o_sqf = alloc(8 * TB * 2)
    SQ_F = view(o_sqf, BF16, (8, TB))
    ACTB4 = view(o_sqf, BF16, (TM + TB,))
    WOX = [view(o_sqf + 5120, BF16, (1024,)), view(alloc(2048), BF16, (1024,))]
    ffn_end = off[0]

    off[0] = uni_base
    o_halo2 = alloc(8 * TB * 4)
    assert o_halo2 == o_halo
    KT = view(o_halo, BF16, (4, 1536))
    MRG = view(o_halo, BF16, (8, 1024))
    o_h2 = alloc(8 * 1536 * 2)
    H2 = view(o_h2, BF16, (8, 1536))
    o_v = alloc(12 * 512 * 2)
    V = view(o_v, BF16, (12, 512))
    o_expb = alloc(8 * TABW * 2)
    EXPB = view(o_expb, BF16, (8, TABW))
    o_reg = alloc(30976)
    TABT = view(o_reg, F32, (8 * TABW,))
    UB = [view(o_reg + 4160 * i, F32, (1040,)) for i in range(2)]
    TT = [view(o_reg + 8320 + 4160 * i, F32, (1040,)) for i in range(2)]
    MIXED = view(o_reg + 16640, BF16, (4, 1024))
    SQK = [view(o_reg + 24832 + 1024 * i, BF16, (TB,)) for i in range(2)]
    RK = [view(o_reg + 26880 + 2048 * i, F32, (TB,)) for i in range(2)]
    ATT = view(o_reg, BF16, (4, 1024))
    PRAW = [view(o_reg + 8192 + 3072 * i, BF16, (1536,)) for i in range(2)]
    PP = [view(o_reg + 14336 + 3072 * i, BF16, (1536,)) for i in range(3)]
    RDEN = [view(o_reg + 23552 + 1024 * i, F32, (256,)) for i in range(2)]
    SG = [view(o_reg + 8192 + 2048 * i, F32, (TB,)) for i in range(4)]
    M1 = [view(o_reg + 16384 + 2048 * i, F32, (TB,)) for i in range(2)]
    QT = view(o_w + 4096 * 6, BF16, (4, 1024))
    YP = view(o_w + 4096 * 8, BF16, (4, 1024))
    SQ_M = view(o_w + 4096 * 6, BF16, (8, TB))
    RSTD_M = [view(o_w + 4096 * 8 + 2048 * i, F32, (TB,)) for i in range(2)]
    TMP_M = [view(o_w + 4096 * 9 + 2048 * i, F32, (TB,)) for i in range(2)]
    mix_end = off[0]
    assert max(ffn_end, mix_end) <= ARENA_BYTES

    BANK = [Res() for _ in range(8)]
    VIEWS = dict(XB=XB, HALO=HALO, H=H, MOD=MOD, DER=DER, CST=CST, ACTB=ACTB, H2=H2, KT=KT, V=V, QT=QT, YP=YP,
                 EXPB=EXPB, ATT=ATT, MRG=MRG, MIXED=MIXED, SCB=SCB)

    def check(name):
        if stop == name:
            S.frozen = True

    def bank(i, n=1):
        return ps[:, i * 512:(i + n) * 512]

    def xblk(b, kt=None):
        if b == 0:
            return HALO[:, :, :] if kt is None else HALO[:, kt, :]
        if kt is None:
            return XB[:, :, (b - 1) * TB:b * TB]
        return XB[:, kt, (b - 1) * TB:b * TB]

    XR = [Res() for _ in range(5)]
    HR = [Res() for _ in range(5)]
    CSTR = Res()
    MODR = Res()
    DERR = Res()
    MISC = Res()

    S.op("sp", lambda e: e.dma_start(out=CST[:, :], in_=cst_d), writes=[CSTR], dma="d_cst")
    xv = xT.rearrange("(k p) t -> p k t", p=128)
    S.op("sp", lambda e: e.dma_start(out=xblk(0), in_=xv[:, :, 0:TB]), writes=[XR[0]], dma="d_x0")

    S.op("dve", lambda e: e.memset(ONES[:, :], 1.0 / 1024.0), writes=[MISC])
    S.op("dve", lambda e: e.memset(BONES[:, :], 0.0), writes=[MISC])
    S.op("dve", lambda e: e.memset(BONES[0:64, 0:64], 1.0 / 64.0), writes=[MISC])
    S.op("dve", lambda e: e.memset(BONES[64:128, 64:128], 1.0 / 64.0), writes=[MISC])
    S.op("dve", lambda e: e.memset(VONES[:, :, :], 1.0), writes=[MISC])
    S.op("dve", lambda e: e.memset(EPSC[:, :], EPS), writes=[MISC])
    S.op("dve", lambda e: e.tensor_scalar(out=VONES[:, 0, :], in0=VONES[:, 0, :], scalar1=CST[:, 110:111],
                                           scalar2=None, op0=ALU.mult), reads=[CSTR, MISC], writes=[MISC])
    S.op("act", lambda e: e.activation(out=SCB[:, :], in_=CST[:, 0:8], func=AF.Silu), reads=[CSTR], writes=[MISC])
    S.op("pool", lambda e: e.dma_start(out=WPG[:, :, :], in_=w_pg.rearrange("(g c) d -> c g d", c=128)),
         writes=[MISC], dma="d_wpg", nobar=True)

    ada_v = w_ada.rearrange("(k p) n -> p k n", p=128)
    MODPS = [bank(7), bank(7)]
    MODBANK = [BANK[7], BANK[7]]

    ADA_DMA = {}

    def ada_chunk(c):
        piece = 4 + (c % 2)
        wt = wpiece(piece, (8, 256))
        ADA_DMA[c] = S.op("pool", lambda e: e.dma_start(out=wt, in_=ada_v[:, :, c * 256:(c + 1) * 256]),
                          writes=[WP[piece]], dma="d_w%d" % piece, nobar=True)
        part = 0 if c < 12 else 1

        def mm(e):
            ins = None
            for jj in range(2):
                j = 2 * c + jj
                for kt in range(8):
                    ins = e.matmul(MODPS[part][:, j:j + 1], lhsT=wt[:, kt, jj * 128:(jj + 1) * 128],
                                   rhs=SCB[:, kt:kt + 1], start=(kt == 0), stop=(kt == 7))
            return ins
        S.op("pe", mm, reads=[WP[piece], MISC], writes=[MODBANK[part]], nobar=True)

    def mod_finish(part, lo, hi, rows):
        S.op("dve", lambda e: e.tensor_tensor(out=MOD[:, lo:hi], in0=MODPS[part][:, lo:hi], in1=CST[:, 8 + lo:8 + hi],
                                               op=ALU.add), reads=[MODBANK[part], CSTR], writes=[MODR], nobar=True)
        for r in rows:
            i, kind = r // 3, r % 3
            if kind == 0:
                gcol = 80 + 8 * i
                S.op("dve", lambda e, i=i, gcol=gcol: e.scalar_tensor_tensor(
                    out=DER[:, 3 * i, :], in0=MOD[:, 24 * i + 8:24 * i + 16], scalar=1.0, in1=CST[:, gcol:gcol + 8],
                    op0=ALU.add, op1=ALU.mult), reads=[MODR, CSTR], writes=[DERR], nobar=True)
            elif kind == 1:
                S.op("dve", lambda e, i=i: e.tensor_copy(out=DER[:, 3 * i + 1, :], in_=MOD[:, 24 * i:24 * i + 8]),
                     reads=[MODR], writes=[DERR], nobar=True)
            else:
                gs = (0.5, 1.0, 0.5)[i]
                S.op("dve", lambda e, i=i, gs=gs: e.tensor_scalar(
                    out=DER[:, 3 * i + 2, :], in0=MOD[:, 24 * i + 16:24 * i + 24], scalar1=gs, scalar2=None,
                    op0=ALU.mult), reads=[MODR], writes=[DERR], nobar=True)

    for c in range(8):
        ada_chunk(c)
    mod_finish(0, 0, 16, [0, 1])
    S.op("sp", lambda e: e.dma_start(out=xblk(1), in_=xv[:, :, TB:2 * TB]),
         writes=[XR[1]], dma="d_x1", deps=[ADA_DMA[5]])
    check("mod0")

    def norm_stats_a(xap_fn, xres, SQb, ssbank, part=None):
        xall = xap_fn(None)
        if part in (None, 0):
            S.op("act", lambda e: e.activation(out=SQb[:, :, :], in_=xall, func=AF.Square),
                 reads=[xres], writes=[NRES["sq"]])
        if part == 0:
            return

        def mm(e):
            ins = None
            for kt in range(8):
                ins = e.matmul(bank(ssbank), lhsT=ONES[:, :], rhs=SQb[:, kt, :], start=(kt == 0), stop=(kt == 7))
            return ins
        S.op("pe", mm, reads=[NRES["sq"], MISC], writes=[BANK[ssbank]])

    def norm_stats_b(RSTDb, rkey, ssbank):
        S.op("act", lambda e: e.activation(out=RSTDb[:, :], in_=bank(ssbank), func=AF.Ln, bias=EPSC[:, 0:1], scale=1.0),
             reads=[BANK[ssbank], MISC], writes=[NRES[rkey]])
        S.op("act", lambda e: e.activation(out=RSTDb[:, :], in_=RSTDb[:, :], func=AF.Exp, scale=-0.5),
             reads=[NRES[rkey]], writes=[NRES[rkey]])

    def norm_apply(xap_fn, xres, sub, out_fn, hres, RSTDb, rkey, TMPb):
        for kt in range(8):
            tb = TMPb[kt % 2]
            tr = NRES["tmp%d" % (kt % 2)]
            S.op("dve", lambda e, kt=kt, tb=tb: e.scalar_tensor_tensor(
                out=tb[:, :], in0=xap_fn(kt), scalar=DER[:, 3 * sub, kt:kt + 1], in1=RSTDb[:, :],
                op0=ALU.mult, op1=ALU.mult), reads=[xres, NRES[rkey], DERR], writes=[tr])
            if kt % 2 == 0:
                S.op("act", lambda e, kt=kt, tb=tb: e.activation(
                    out=out_fn(kt), in_=tb[:, :], func=AF.Identity, bias=DER[:, 3 * sub + 1, kt:kt + 1], scale=1.0),
                    reads=[tr, DERR], writes=[hres])
            else:
                S.op("dve", lambda e, kt=kt, tb=tb: e.tensor_scalar(
                    out=out_fn(kt), in0=tb[:, :], scalar1=DER[:, 3 * sub + 1, kt:kt + 1], scalar2=None, op0=ALU.add),
                    reads=[tr, DERR], writes=[hres])

    norm_ctr = [0]

    def norm_thunks(items, sub, SQb, RSTDs, TMPb, ssbank, all_dve=False):
        out = []
        for (xf, xr, of, hr) in items:
            i = norm_ctr[0]
            norm_ctr[0] += 1
            rb, rk = RSTDs[i % 2], "rstd%d" % (i % 2)
            th = [lambda xf=xf, xr=xr: norm_stats_a(xf, xr, SQb, ssbank, 0),
                  lambda xf=xf, xr=xr: norm_stats_a(xf, xr, SQb, ssbank, 1),
                  lambda rb=rb, rk=rk: norm_stats_b(rb, rk, ssbank)]
            for kt in range(8):
                th.append(lambda xf=xf, xr=xr, of=of, hr=hr, rb=rb, rk=rk, kt=kt:
                          norm_apply_kt(xf, xr, sub, of, hr, rb, rk, TMPb, kt, all_dve))
            out.append(th)
        return out

    def norm_apply_kt(xap_fn, xres, sub, out_fn, hres, RSTDb, rkey, TMPb, kt, all_dve=False):
        tb = TMPb[kt % 2]
        tr = NRES["tmp%d" % (kt % 2)]
        S.op("dve", lambda e: e.scalar_tensor_tensor(
            out=tb[:, :], in0=xap_fn(kt), scalar=DER[:, 3 * sub, kt:kt + 1], in1=RSTDb[:, :],
            op0=ALU.mult, op1=ALU.mult), reads=[xres, NRES[rkey], DERR], writes=[tr])
        if kt % 2 == 0 and not all_dve:
            S.op("act", lambda e: e.activation(
                out=out_fn(kt), in_=tb[:, :], func=AF.Identity, bias=DER[:, 3 * sub + 1, kt:kt + 1], scale=1.0),
                reads=[tr, DERR], writes=[hres])
        else:
            S.op("dve", lambda e: e.tensor_scalar(
                out=out_fn(kt), in0=tb[:, :], scalar1=DER[:, 3 * sub + 1, kt:kt + 1], scalar2=None, op0=ALU.add),
                reads=[tr, DERR], writes=[hres])

    def norm_seq(items, sub, SQb, RSTDs, TMPb, ssbank, extra=None, all_dve=False):
        ths = norm_thunks(items, sub, SQb, RSTDs, TMPb, ssbank, all_dve)
        extra = list(extra or [])
        n = len(ths)
        if n == 0:
            return
        ths[0][0]()
        ths[0][1]()
        ths[0][2]()
        for k in range(n):
            if k + 1 < n:
                ths[k + 1][0]()
                ths[k + 1][1]()
            for t in ths[k][3:]:
                t()
                if k >= 1 and extra:
                    extra.pop(0)()
            if k + 1 < n:
                ths[k + 1][2]()
        for t in extra:
            t()

    NRES = {k: Res() for k in ("sq", "rstd0", "rstd1", "tmp0", "tmp1")}

    outv = outT.rearrange("(k p) t -> p k t", p=128)

    def ffn(w_i, w_o_, sub, blocks, first):
        wi_v = w_i.rearrange("(k p) n -> p k n", p=128)
        wo_v = w_o_.rearrange("(k p) n -> p k n", p=128)
        pre = {}
        if first:
            wa0 = wpiece(0, (8, 256))
            wb0 = wpiece(1, (8, 256))
            S.op("pool", lambda e: e.dma_start(out=wa0, in_=wi_v[:, :, 0:256]),
                 writes=[WP[0]], dma="d_w0", nobar=True)
            S.op("pool", lambda e: e.dma_start(out=wb0, in_=wi_v[:, :, DFF:DFF + 256]),
                 writes=[WP[1]], dma="d_w1", nobar=True)
            pre[0] = True
            wlast = S.q["pool"][-1]
            for b in range(2, 5):
                S.op("sp", lambda e, b=b: e.dma_start(out=xblk(b), in_=xv[:, :, b * TB:(b + 1) * TB]),
                     writes=[XR[b]], dma="d_x%d" % b, deps=[wlast])

        def nitem(b):
            return (lambda kt, b=b: xblk(b, kt), XR[b], lambda kt, b=b: H[:, kt, b * TB:(b + 1) * TB], HR[b])
        pending_norms = list(blocks)
        norm_seq([nitem(pending_norms.pop(0)), nitem(pending_norms.pop(0))], sub, SQ_F, RSTD, TMP, 6)
        ACTR = {}
        SAR = [Res(), Res()]
        cnt = [0]
        ada_next = [8] if first else [36]
        GSIZES = [4, 4, 4, 5, 5]
        WOXR = [Res(), Res()]
        GSTART = [sum(GSIZES[:g]) for g in range(len(GSIZES))]
        ngroups = len(GSIZES)
        chunk_ctr = [0]

        def wi_load(g, c2):
            j0 = GSTART[g] + 2 * c2
            nt = min(2, GSTART[g] + GSIZES[g] - j0)
            sl = chunk_ctr[0] % 2
            chunk_ctr[0] += 1
            wa = wpiece(2 * sl, (8, 256))
            wb = wpiece(2 * sl + 1, (8, 256))
            if not (g == 0 and c2 == 0 and pre.get(0)):
                S.op("pool", lambda e: e.dma_start(out=wa[:, :, 0:nt * 128], in_=wi_v[:, :, j0 * 128:(j0 + nt) * 128]),
                     writes=[WP[2 * sl]], dma="d_w%d" % (2 * sl), nobar=True)
                S.op("pool", lambda e: e.dma_start(out=wb[:, :, 0:nt * 128],
                                                   in_=wi_v[:, :, DFF + j0 * 128:DFF + (j0 + nt) * 128]),
                     writes=[WP[2 * sl + 1]], dma="d_w%d" % (2 * sl + 1), nobar=True)
            return sl, wa, wb, nt

        def actb(jl, b):
            if jl == 4:
                return ACTB4[:, b * TB:(b + 1) * TB]
            return ACTB[:, jl, b * TB:(b + 1) * TB]

        def unit(g, c2, jj, b, sl, wa, wb):
            jl = 2 * c2 + jj
            i = cnt[0]
            cnt[0] += 1
            ba, bb = i % 2, 2 + i % 2
            hb = H[:, :, b * TB:(b + 1) * TB]

            def mma(e):
                ins = None
                for kt in range(8):
                    ins = e.matmul(bank(ba), lhsT=wa[:, kt, jj * 128:(jj + 1) * 128], rhs=hb[:, kt, :],
                                   start=(kt == 0), stop=(kt == 7))
                return ins

            def mmb(e):
                ins = None
                for kt in range(8):
                    ins = e.matmul(bank(bb), lhsT=wb[:, kt, jj * 128:(jj + 1) * 128], rhs=hb[:, kt, :],
                                   start=(kt == 0), stop=(kt == 7))
                return ins
            S.op("pe", mma, reads=[WP[2 * sl], HR[b]], writes=[BANK[ba]])
            S.op("pe", mmb, reads=[WP[2 * sl + 1], HR[b]], writes=[BANK[bb]])
            S.op("act", lambda e: e.activation(out=SA[i % 2][:, :], in_=bank(ba), func=AF.Silu),
                 reads=[BANK[ba]], writes=[SAR[i % 2]])
            ar = ACTR.setdefault((jl, b), Res())
            S.op("dve", lambda e: e.tensor_tensor(
                out=actb(jl, b), in0=SA[i % 2][:, :], in1=bank(bb), op=ALU.mult),
                reads=[SAR[i % 2], BANK[bb]], writes=[ar],
                deps=([NRES["sq"].w] + list(NRES["sq"].r.values())) if jl == 4 else ())
            if ada_next[0] < 36 and ((ada_next[0] < 12 and i >= 8) or (ada_next[0] >= 12 and i % 3 == 2)):
                ada_chunk(ada_next[0])
                ada_next[0] += 1
                if ada_next[0] == 12:
                    mod_finish(0, 16, 24, [2])
                if ada_next[0] == 36:
                    mod_finish(1, 24, 72, [3, 4, 5, 6, 7, 8])

        for g in range(ngroups):
            nk = GSIZES[g]
            nch = (nk + 1) // 2
            if g == 0:
                chunks = [wi_load(g, c2) for c2 in range(nch)]
                nth_ = []
                carry, carry_next = [], []
                for b in blocks:
                    u = 0
                    for t in carry:
                        t()
                    carry, carry_next = carry_next, []
                    for c2 in range(nch):
                        sl_, wa_, wb_, nt_ = chunks[c2]
                        for jj in range(nt_):
                            unit(g, c2, jj, b, sl_, wa_, wb_)
                            u += 1
                            if u == 3 and pending_norms:
                                nth_ = norm_thunks([nitem(pending_norms.pop(0))], sub, SQ_F, RSTD, TMP, 6, False)[0]
                                nth_[0]()
                            elif u == 4 and nth_:
                                nth_[1]()
                                nth_[2]()
                                carry_next = nth_[3:]
                                nth_ = []
                            for t in carry[:2]:
                                t()
                            carry = carry[2:]
                for t in carry + carry_next:
                    t()
            else:
                for c2 in range(nch):
                    sl_, wa_, wb_, nt_ = wi_load(g, c2)
                    for jj in range(nt_):
                        for b in blocks:
                            unit(g, c2, jj, b, sl_, wa_, wb_)
            so = g % 2
            wo = view(o_w + 4096 * (6 + 2 * so), BF16, (4, 1024))
            nk4 = min(nk, 4)
            S.op("pool", lambda e, wo=wo, g=g, nk4=nk4: e.dma_start(out=wo[:, 0:nk4, :],
                                                                    in_=wo_v[:, GSTART[g]:GSTART[g] + nk4, :]),
                 writes=[WP[6 + 2 * so], WP[7 + 2 * so]], dma="d_w%d" % (6 + 2 * so))
            wox = WOX[so]
            if nk == 5:
                S.op("pool", lambda e, wox=wox, g=g: e.dma_start(out=wox[:, :], in_=wo_v[:, GSTART[g] + 4, :]),
                     writes=[WOXR[so]], dma="d_wox%d" % so,
                     deps=[NRES["sq"].w] + list(NRES["sq"].r.values()))
            last = (g == ngroups - 1)
            order = ([(b, n) for b in blocks for n in range(0, 8, 2)] if last
                     else [(b, n) for n in range(0, 8, 2) for b in blocks])
            k = 0
            for (b, n) in order:
                ob = (4, 0, 2)[k % 3]
                k += 1

                def mmo(e, wo=wo, wox=wox, n=n, b=b, ob=ob, nk=nk):
                    ins = None
                    for dn in range(2):
                        cs = slice((n + dn) * 128, (n + dn + 1) * 128)
                        for jl in range(nk):
                            lw = wox[:, cs] if jl == 4 else wo[:, jl, cs]
                            ins = e.matmul(bank(ob + dn), lhsT=lw, rhs=actb(jl, b),
                                           start=(jl == 0), stop=(jl == nk - 1))
                    return ins
                S.op("pe", mmo, reads=[WP[6 + 2 * so], WP[7 + 2 * so]] + ([WOXR[so]] if nk == 5 else [])
                     + [ACTR[(jl, b)] for jl in range(nk)], writes=[BANK[ob], BANK[ob + 1]])
                evs = []
                for dn in range(2):
                    evs.append(S.op("dve", lambda e, n=n + dn, b=b, obb=ob + dn: e.scalar_tensor_tensor(
                        out=xblk(b, n), in0=bank(obb), scalar=DER[:, 3 * sub + 2, n:n + 1], in1=xblk(b, n),
                        op0=ALU.mult, op1=ALU.add), reads=[BANK[ob + dn], DERR], writes=[XR[b]]))
                if last and not first:
                    S.op("sp", lambda e, n=n, b=b: e.dma_start(
                        out=outv[:, n:n + 2, (b - 1) * TB:b * TB], in_=XB[:, n:n + 2, (b - 1) * TB:b * TB]),
                        deps=evs, dma="d_st", force=True)

    ffn(w1i, w1o, 0, [0, 1, 2, 3, 4], True)
    check("ffn1")
    S.barrier()

    TABR = Res()
    S.op("sp", lambda e: e.dma_start(out=TABT[:, :], in_=tab_d), writes=[TABR], dma="d_tab")
    EXPR = Res()

    win_v = w_in.rearrange("(k p) n -> p k n", p=128)
    ring = [0]
    RING = [0, 1, 2, 3, 4, 5]

    def load_w(src_ap, shape):
        piece = RING[ring[0] % 6]
        ring[0] += 1
        wt = wpiece(piece, shape)
        S.op("pool", lambda e: e.dma_start(out=wt, in_=src_ap), writes=[WP[piece]], dma="d_w%d" % piece, nobar=True)
        return wt, WP[piece]

    wao_v = w_ao.rearrange("(k p) n -> p k n", p=128)
    wpo_v = w_po.rearrange("(k p) n -> p k n", p=128)
    wo_v2 = w_o.rearrange("(k p) n -> p k n", p=128)
    expb_pstride = EXPB[:, 0, :].ap[0][0]

    for pas in (0, 1):
        eblk = [2 * pas + e for e in range(3)]
        if pas == 0:
            items = []
            for e_i, b in enumerate(eblk):
                items.append((lambda kt, b=b: xblk(b, kt), XR[b],
                              lambda kt, e_i=e_i: H2[:, kt, e_i * TB:(e_i + 1) * TB], HR[e_i]))
            tab_th = [lambda h=h: S.op("act", lambda e: e.activation(
                out=EXPB[:, h, :], in_=TABT[:, h * TABW:(h + 1) * TABW], func=AF.Exp),
                reads=[TABR], writes=[EXPR]) for h in range(8)]
            norm_seq(items, 1, SQ_M, RSTD_M, TMP_M, 6, extra=tab_th)
        check("s1_%d" % pas)
        S.barrier()
        KR = [Res() for _ in range(3)]
        QR = Res()
        VR = [Res() for _ in range(12)]
        SQKR = [Res(), Res()]
        RKR = [Res(), Res()]
        cnt = 0
        UR = [Res(), Res()]
        TR = [Res(), Res()]
        MXR = [Res() for _ in range(4)]
        YPR = Res()
        uw = [load_w(win_v[:, :, 1536 + c * 256:1536 + (c + 1) * 256], (8, 256)) for c in range(2)]

        def u_proj(gq):
            wt, wr = uw[gq // 2]
            f = gq % 2
            ub = UB[gq % 2]
            ur = UR[gq % 2]
            for e_i in range(3):
                ubk = 6 + (3 * gq + e_i) % 2
                if e_i == 0:
                    rhs_fn = lambda kt: H2[:, kt, TB - 64:TB]
                    ncol = 64
                else:
                    rhs_fn = lambda kt, e_i=e_i: H2[:, kt, e_i * TB:(e_i + 1) * TB]
                    ncol = TB

                def mmu(e, rhs_fn=rhs_fn, ubk=ubk, ncol=ncol):
                    ins = None
                    for kt in range(8):
                        ins = e.matmul(bank(ubk)[:, 0:ncol], lhsT=wt[:, kt, f * 128:(f + 1) * 128], rhs=rhs_fn(kt),
                                       start=(kt == 0), stop=(kt == 7))
                    return ins
                S.op("pe", mmu, reads=[wr, HR[e_i]], writes=[BANK[ubk]])
                if e_i == 0:
                    if pas == 0:
                        S.op("dve", lambda e, ubk=ubk: e.tensor_scalar(
                            out=ub[:, 0:16], in0=bank(ubk)[:, 48:64], scalar1=CST[:, 110:111], scalar2=None,
                            op0=ALU.mult), reads=[BANK[ubk], CSTR], writes=[ur])
                    else:
                        S.op("dve", lambda e, ubk=ubk: e.tensor_copy(out=ub[:, 0:16], in_=bank(ubk)[:, 48:64]),
                             reads=[BANK[ubk]], writes=[ur])
                else:
                    S.op("act", lambda e, ubk=ubk, e_i=e_i: e.activation(
                        out=ub[:, 16 + (e_i - 1) * TB:16 + e_i * TB], in_=bank(ubk), func=AF.Copy),
                        reads=[BANK[ubk]], writes=[ur])

        def u_pool(gq):
            ub = UB[gq % 2]
            ur = UR[gq % 2]
            th = []
            nsteps = gq + 1
            src, sres = ub, ur
            for st in range(nsteps):
                sh = 1 << st
                dst, dres = TT[st % 2], TR[st % 2]
                lo = 2 * sh - 1
                th.append(lambda src=src, dst=dst, sh=sh, lo=lo, sres=sres, dres=dres: S.op(
                    "dve", lambda e: e.tensor_tensor(
                        out=dst[:, lo:1040], in0=src[:, lo:1040], in1=src[:, lo - sh:1040 - sh], op=ALU.add),
                    reads=[sres], writes=[dres]))
                src, sres = dst, dres
            w = 2 << gq
            th.append(lambda src=src, sres=sres: S.op("dve", lambda e: e.scalar_tensor_tensor(
                out=MIXED[:, gq, :], in0=src[:, 16:1040], scalar=1.0 / w, in1=ub[:, 16:1040],
                op0=ALU.mult, op1=ALU.subtract), reads=[sres, ur], writes=[MXR[gq]]))
            if pas == 0:
                t16 = TT[(nsteps) % 2]
                t16r = TR[(nsteps) % 2]
                th.append(lambda src=src, sres=sres: S.op("dve", lambda e: e.tensor_tensor(
                    out=t16[:, 0:16], in0=src[:, 16:32], in1=CST[:, 112 + 16 * gq:128 + 16 * gq], op=ALU.mult),
                    reads=[sres, CSTR], writes=[t16r]))
                th.append(lambda: S.op("dve", lambda e: e.tensor_tensor(
                    out=MIXED[:, gq, 0:16], in0=t16[:, 0:16], in1=ub[:, 16:32], op=ALU.subtract),
                    reads=[t16r, ur], writes=[MXR[gq]]))
            return th

        def u_yp(gq):
            for mb in range(2):
                ypb = 4 + mb
                S.op("pe", lambda e, mb=mb, ypb=ypb: e.matmul(
                    bank(ypb), lhsT=WPG[:, gq, :], rhs=MIXED[:, gq, mb * TB:(mb + 1) * TB], start=True, stop=True),
                    reads=[MXR[gq], MISC], writes=[BANK[ypb]])
                S.op("dve", lambda e, mb=mb, ypb=ypb: e.tensor_scalar(
                    out=YP[:, gq, mb * TB:(mb + 1) * TB], in0=bank(ypb), scalar1=CST[:, 106 + gq:107 + gq],
                    scalar2=None, op0=ALU.mult), reads=[BANK[ypb], CSTR], writes=[YPR])

        u_proj(0)
        u_proj(1)
        pool_q = u_pool(0) + u_pool(1)
        units = []
        for which, col0, eb_list, gcol in (("k", 512, [0, 1, 2], 105), ("q", 0, [1, 2], 104)):
            for c in range(2):
                wt, wr = load_w(win_v[:, :, col0 + c * 256:col0 + (c + 1) * 256], (8, 256))
                for f in range(2):
                    for e_i in eb_list:
                        units.append((which, wt, wr, f, 2 * c + f, e_i, gcol))

        def kq_raw(i):
            which, wt, wr, f, ft, e_i, gcol = units[i]
            pb = i % 4
            hb = H2[:, :, e_i * TB:(e_i + 1) * TB]

            def mmk(e):
                ins = None
                for kt in range(8):
                    ins = e.matmul(bank(pb), lhsT=wt[:, kt, f * 128:(f + 1) * 128], rhs=hb[:, kt, :],
                                   start=(kt == 0), stop=(kt == 7))
                return ins
            S.op("pe", mmk, reads=[wr, HR[e_i]], writes=[BANK[pb]])
            S.op("act", lambda e: e.activation(out=SQK[i % 2][:, :], in_=bank(pb), func=AF.Square),
                 reads=[BANK[pb]], writes=[SQKR[i % 2]])

        def kq_rest(i):
            which, wt, wr, f, ft, e_i, gcol = units[i]
            pb = i % 4
            sb_ = 4 + i % 2
            S.op("pe", lambda e: e.matmul(bank(sb_), lhsT=BONES[:, :], rhs=SQK[i % 2][:, :], start=True, stop=True),
                 reads=[SQKR[i % 2], MISC], writes=[BANK[sb_]])
            S.op("act", lambda e: e.activation(out=RK[i % 2][:, :], in_=bank(sb_), func=AF.Ln, bias=EPSC[:, 0:1],
                                               scale=1.0), reads=[BANK[sb_], MISC], writes=[RKR[i % 2]])
            S.op("act", lambda e: e.activation(out=RK[i % 2][:, :], in_=RK[i % 2][:, :], func=AF.Exp, scale=-0.5),
                 reads=[RKR[i % 2]], writes=[RKR[i % 2]])
            if which == "k":
                dst = KT[:, ft, e_i * TB:(e_i + 1) * TB]
                dres = KR[e_i]
            else:
                dst = QT[:, ft, (e_i - 1) * TB:e_i * TB]
                dres = QR
            S.op("dve", lambda e: e.scalar_tensor_tensor(
                out=dst, in0=bank(pb), scalar=CST[:, gcol:gcol + 1], in1=RK[i % 2][:, :],
                op0=ALU.mult, op1=ALU.mult), reads=[BANK[pb], RKR[i % 2], CSTR], writes=[dres])

        kq_raw(0)
        for i in range(len(units)):
            if i + 1 < len(units):
                kq_raw(i + 1)
            kq_rest(i)
            if pool_q:
                pool_q.pop(0)()
            if i == 9:
                while pool_q:
                    pool_q.pop(0)()
                u_proj(2)
                u_proj(3)
                pool_q = u_pool(2) + u_pool(3)
        while pool_q:
            pool_q.pop(0)()
        wv0, wvr0 = load_w(win_v[:, :, 1024:1280], (8, 256))
        wv1, wvr1 = load_w(win_v[:, :, 1280:1536], (8, 256))
        for t in range(12):
            vb = 4 + t % 2

            def mmv(e, t=t, vb=vb, wv0=wv0, wv1=wv1):
                ins = None
                for half, wv in ((0, wv0), (1, wv1)):
                    for kt in range(8):
                        ins = e.matmul(bank(vb)[:, half * 256:(half + 1) * 256], lhsT=H2[:, kt, t * 128:(t + 1) * 128],
                                       rhs=wv[:, kt, :], start=(kt == 0), stop=(kt == 7))
                return ins
            S.op("pe", mmv, reads=[wvr0, wvr1, HR[t // 4]], writes=[BANK[vb]])
            if pas == 0 and t < 4:
                S.op("dve", lambda e, t=t, vb=vb: e.tensor_scalar(out=V[:, t, :], in0=bank(vb), scalar1=CST[:, 110:111],
                                                                  scalar2=None, op0=ALU.mult),
                     reads=[BANK[vb], CSTR], writes=[VR[t]])
            else:
                S.op("act", lambda e, t=t, vb=vb: e.activation(out=V[:, t, :], in_=bank(vb), func=AF.Copy),
                     reads=[BANK[vb]], writes=[VR[t]])
        for gq in range(4):
            u_yp(gq)
        check("s2_%d" % pas)
        S.barrier()
        PRR = [Res(), Res()]
        PPR = [Res(), Res(), Res()]
        RDR = [Res(), Res()]
        ATTR = Res()
        def att_front(u):
            qb, h = u // 8, u % 8
            s = u % 2
            ft, po = h // 2, (h % 2) * 64
            sb0 = 3 * s

            def mms(e):
                ins = None
                for jj in range(6):
                    kt_ = 2 * qb + 5 - jj
                    ins = e.matmul(ps[:, sb0 * 512 + jj * 256:sb0 * 512 + (jj + 1) * 256],
                                   lhsT=KT[po:po + 64, ft, kt_ * 128:(kt_ + 1) * 128],
                                   rhs=QT[po:po + 64, ft, qb * 256:(qb + 1) * 256], start=True, stop=True)
                return ins
            S.op("pe", mms, reads=KR + [QR], writes=[BANK[sb0], BANK[sb0 + 1], BANK[sb0 + 2]])
            S.op("act", lambda e: e.activation(out=PRAW[s][:, :], in_=bank(sb0, 3), func=AF.Exp, scale=0.125),
                 reads=[BANK[sb0], BANK[sb0 + 1], BANK[sb0 + 2]], writes=[PRR[s]])
            ewin = bass.AP(EXPB.tensor, EXPB[:, h, 0:1].offset, [[expb_pstride, 128], [128, 6], [1, 256]])
            S.op("dve", lambda e: e.tensor_tensor(
                out=PP[u % 3][:, :].rearrange("p (a b) -> p a b", a=6),
                in0=PRAW[s][:, :].rearrange("p (a b) -> p a b", a=6), in1=ewin, op=ALU.mult),
                reads=[PRR[s], EXPR], writes=[PPR[u % 3]])

        def att_back(u, pas=pas):
            qb, h = u // 8, u % 8
            s = u % 2
            ft, po = h // 2, (h % 2) * 64
            pr = (u // 2) % 2
            ndb = 6 + pr

            def mmpv(e):
                ins = None
                for jj in range(6):
                    kt_ = 2 * qb + 5 - jj
                    ins = e.matmul(bank(ndb)[po:po + 64, 0:256], lhsT=V[:, kt_, h * 64:(h + 1) * 64],
                                   rhs=PP[u % 3][:, jj * 256:(jj + 1) * 256], start=(jj == 0), stop=(jj == 5))
                for jj in range(6):
                    kt_ = 2 * qb + 5 - jj
                    var = 0 if (pas == 0 and kt_ < 4) else 1
                    ins = e.matmul(bank(ndb)[po:po + 64, 256:512], lhsT=VONES[:, var, :],
                                   rhs=PP[u % 3][:, jj * 256:(jj + 1) * 256], start=(jj == 0), stop=(jj == 5))
                return ins
            S.op("pe", mmpv, reads=VR + [PPR[u % 3], MISC], writes=[BANK[ndb]])
            if h % 2 == 1:
                S.op("act", lambda e: e.activation(out=RDEN[pr][:, :], in_=bank(ndb)[:, 256:512], func=AF.Ln),
                     reads=[BANK[ndb]], writes=[RDR[pr]])
                S.op("act", lambda e: e.activation(out=RDEN[pr][:, :], in_=RDEN[pr][:, :], func=AF.Exp, scale=-1.0),
                     reads=[RDR[pr]], writes=[RDR[pr]])
                S.op("dve", lambda e: e.tensor_tensor(
                    out=ATT[:, ft, qb * 256:(qb + 1) * 256], in0=bank(ndb)[:, 0:256],
                    in1=RDEN[pr][:, :], op=ALU.mult), reads=[BANK[ndb], RDR[pr]], writes=[ATTR])

        att_front(0)
        att_front(1)
        for u in range(32):
            if u + 2 < 32:
                att_front(u + 2)
            att_back(u)
        check("s3_%d" % pas)
        S.barrier()
        SGR = [Res() for _ in range(4)]
        M1R = [Res(), Res()]
        MRGR = Res()
        it = 0
        for c in range(4):
            wga, rga = load_w(win_v[:, :, 2048 + c * 256:2048 + (c + 1) * 256], (8, 256))
            wgb, rgb = load_w(win_v[:, :, 3072 + c * 256:3072 + (c + 1) * 256], (8, 256))
            wao, rao = load_w(wao_v[:, :, c * 256:(c + 1) * 256], (4, 256))
            wpo, rpo = load_w(wpo_v[:, :, c * 256:(c + 1) * 256], (4, 256))
            for f in range(2):
                n = 2 * c + f
                for mb in range(2):
                    s = it % 2
                    it += 1
                    b0 = 4 * s
                    hb = H2[:, :, (1 + mb) * TB:(2 + mb) * TB]
                    fs = slice(f * 128, (f + 1) * 128)
                    ms = slice(mb * TB, (mb + 1) * TB)

                    def mm_ya(e, wao=wao, fs=fs, ms=ms, b0=b0):
                        ins = None
                        for hp in range(4):
                            ins = e.matmul(bank(b0), lhsT=wao[:, hp, fs], rhs=ATT[:, hp, ms], start=(hp == 0), stop=(hp == 3))
                        return ins

                    def mm_yb(e, wpo=wpo, fs=fs, ms=ms, b0=b0):
                        ins = None
                        for g in range(4):
                            ins = e.matmul(bank(b0 + 1), lhsT=wpo[:, g, fs], rhs=YP[:, g, ms], start=(g == 0), stop=(g == 3))
                        return ins

                    def mm_ga(e, wga=wga, fs=fs, hb=hb, b0=b0):
                        ins = None
                        for kt in range(8):
                            ins = e.matmul(bank(b0 + 2), lhsT=wga[:, kt, fs], rhs=hb[:, kt, :], start=(kt == 0), stop=(kt == 7))
                        return ins

                    def mm_gb(e, wgb=wgb, fs=fs, hb=hb, b0=b0):
                        ins = None
                        for kt in range(8):
                            ins = e.matmul(bank(b0 + 3), lhsT=wgb[:, kt, fs], rhs=hb[:, kt, :], start=(kt == 0), stop=(kt == 7))
                        return ins
                    S.op("pe", mm_ga, reads=[rga, HR[1 + mb]], writes=[BANK[b0 + 2]])
                    S.op("pe", mm_gb, reads=[rgb, HR[1 + mb]], writes=[BANK[b0 + 3]])
                    S.op("pe", mm_ya, reads=[rao, ATTR], writes=[BANK[b0]])
                    S.op("pe", mm_yb, reads=[rpo, YPR], writes=[BANK[b0 + 1]])
                    S.op("act", lambda e, s=s, b0=b0: e.activation(out=SG[2 * s][:, :], in_=bank(b0 + 2), func=AF.Sigmoid),
                         reads=[BANK[b0 + 2]], writes=[SGR[2 * s]])
                    S.op("act", lambda e, s=s, b0=b0: e.activation(out=SG[2 * s + 1][:, :], in_=bank(b0 + 3), func=AF.Sigmoid),
                         reads=[BANK[b0 + 3]], writes=[SGR[2 * s + 1]])
                    S.op("dve", lambda e, s=s, b0=b0: e.tensor_tensor(out=M1[s][:, :], in0=bank(b0), in1=SG[2 * s][:, :],
                                                                      op=ALU.mult),
                         reads=[BANK[b0], SGR[2 * s]], writes=[M1R[s]])
                    S.op("dve", lambda e, s=s, b0=b0: e.tensor_tensor(out=SG[2 * s + 1][:, :], in0=bank(b0 + 1),
                                                                      in1=SG[2 * s + 1][:, :], op=ALU.mult),
                         reads=[BANK[b0 + 1], SGR[2 * s + 1]], writes=[SGR[2 * s + 1]])
                    S.op("dve", lambda e, s=s, n=n, ms=ms: e.tensor_tensor(out=MRG[:, n, ms], in0=M1[s][:, :],
                                                                           in1=SG[2 * s + 1][:, :], op=ALU.add),
                         reads=[M1R[s], SGR[2 * s + 1]], writes=[MRGR])
        check("s4_%d" % pas)
        S.barrier()
        side = []
        if pas == 0:
            side.append(lambda: S.op("dve", lambda e: e.tensor_copy(out=H2[:, :, 0:TB], in_=H2[:, :, 2 * TB:3 * TB]),
                                     reads=[HR[2]], writes=[HR[0]]))
            its = [(lambda kt, b=b: xblk(b, kt), XR[b],
                    lambda kt, e_i=e_i: H2[:, kt, e_i * TB:(e_i + 1) * TB], HR[e_i]) for e_i, b in ((1, 3), (2, 4))]
            nt = norm_thunks(its, 1, SQ_M, RSTD_M, TMP_M, 6)
            side += nt[0][:3] + nt[1][:2] + nt[0][3:] + [nt[1][2]] + nt[1][3:]
        it = 0
        for c in range(4):
            wo_, ro_ = load_w(wo_v2[:, :, c * 256:(c + 1) * 256], (8, 256))
            for f in range(2):
                n = 2 * c + f
                for mb in range(2):
                    ob = it % 2
                    it += 1
                    b = 2 * pas + 1 + mb
                    ms = slice(mb * TB, (mb + 1) * TB)

                    def mm_o(e, wo_=wo_, f=f, ms=ms, ob=ob):
                        ins = None
                        for kt in range(8):
                            ins = e.matmul(bank(ob), lhsT=wo_[:, kt, f * 128:(f + 1) * 128], rhs=MRG[:, kt, ms],
                                           start=(kt == 0), stop=(kt == 7))
                        return ins
                    S.op("pe", mm_o, reads=[ro_, MRGR], writes=[BANK[ob]])
                    S.op("dve", lambda e, n=n, b=b, ob=ob: e.scalar_tensor_tensor(
                        out=xblk(b, n), in0=bank(ob), scalar=DER[:, 5, n:n + 1], in1=xblk(b, n),
                        op0=ALU.mult, op1=ALU.add), reads=[BANK[ob], DERR], writes=[XR[b]])
                    for t in side[:2]:
                        t()
                    side = side[2:]
        for t in side:
            t()
        check("s5_%d" % pas)
        S.barrier()

    ffn(w3i, w3o, 2, [1, 2, 3, 4], False)
    STR = Res()
    if S.frozen:
        for b in range(1, 5):
            S.op("sp", lambda e, b=b: e.dma_start(out=outv[:, :, (b - 1) * TB:b * TB], in_=xblk(b)),
                 reads=[XR[b]], writes=[STR], dma="d_st", force=True)
    if dbg_d is not None:
        S.op("pool", lambda e: e.dma_start(out=(dbg_d if len(debug[0](VIEWS).shape) == 2 else dbg_d.rearrange("p (a b) -> p a b", a=debug[0](VIEWS).shape[1])), in_=debug[0](VIEWS)), reads=[], writes=[STR], dma="d_dbg",
             deps=[S.q[x][-1] for x in ("pe", "act", "dve") if S.q[x]], force=True)
        dbg_op = S.q["pool"][-1]
    final = S.q["sp"][-1]

    keys = S.finalize()
    total_st = final.count
    sem_cms = {k: nc.semaphore(k) for k in keys}
    sems = {k: cm.__enter__() for k, cm in sem_cms.items()}
    with nc.Block() as block:
        @block.sync
        def _(e):
            S.emit("sp", e, sems)
            e.wait_ge(sems["d_st"], total_st)
            if dbg_d is not None:
                e.wait_ge(sems["d_dbg"], 16)

        @block.gpsimd
        def _(e):
            S.emit("pool", e, sems)

        @block.tensor
        def _(e):
            S.emit("pe", e, sems)

        @block.scalar
        def _(e):
            S.emit("act", e, sems)

        @block.vector
        def _(e):
            S.emit("dve", e, sems)
    for cm in sem_cms.values():
        cm.__exit__(None, None, None)
    psum_cm.__exit__(None, None, None)
    arena_cm.__exit__(None, None, None)
    return nc


def _prep(inp):
    f = lambda k: np.asarray(inp[k], dtype=np.float32)
    x = f("x")
    c = f("c")
    rel_bias = f("rel_bias")[0]
    k = np.arange(128)[:, None]
    i = np.arange(TABW)[None, :]
    d = i - 128 - k
    cd = i // 64 - 2 - k // 64
    valid = (cd >= 0) & (cd <= 8)
    idx = np.clip(d, -128, 128) + 128
    tab = np.empty((128, 8, TABW), np.float32)
    for h in range(8):
        tab[:, h, :] = np.where(valid, rel_bias[h][idx], np.float32(-30000.0))
    tab = np.ascontiguousarray(tab.reshape(128, 8 * TABW))
    shared = {
        "w_ada": np.ascontiguousarray(f("w_ada")[0]),
        "w1i": np.ascontiguousarray(f("w_ffn1_in")[0]),
        "w1o": np.ascontiguousarray(f("w_ffn1_out")[0]),
        "w3i": np.ascontiguousarray(f("w_ffn2_in")[0]),
        "w3o": np.ascontiguousarray(f("w_ffn2_out")[0]),
        "w_in": np.ascontiguousarray(f("w_in")[0]),
        "tab": tab,
        "w_ao": np.ascontiguousarray(f("w_attn_out")[0]),
        "w_pg": np.ascontiguousarray(f("w_pool_group")[0].reshape(512, 128)),
        "w_po": np.ascontiguousarray(f("w_pool_out")[0]),
        "w_o": np.ascontiguousarray(f("w_o")[0]),
    }
    col = lambda v: v.reshape(-1, 128).T
    in_maps = []
    wins = (2, 4, 8, 16)
    for core in range(8):
        b, s = core // 4, core % 4
        xe = np.zeros((TM + TB, D), np.float32)
        if s == 0:
            xe[TB:] = x[b, 0:TM]
        else:
            xe[:] = x[b, s * TM - TB:(s + 1) * TM]
        cst = np.zeros((128, NCST), np.float32)
        cst[:, 0:8] = col(c[b])
        cst[:, 8:80] = col(f("b_ada")[0])
        cst[:, 80:88] = col(f("g_ffn1")[0])
        cst[:, 88:96] = col(f("g_mix")[0])
        cst[:, 96:104] = col(f("g_ffn2")[0])
        cst[:, 104] = np.tile(f("q_gain")[0], 2)
        cst[:, 105] = np.tile(f("k_gain")[0], 2)
        cst[:, 106:110] = col(f("pool_scale")[0])
        cst[:, 110] = 0.0 if s == 0 else 1.0
        cst[:, 111] = 1.0
        for g in range(4):
            for t in range(16):
                cnt = min(t + 1, wins[g]) if s == 0 else wins[g]
                cst[:, 112 + 16 * g + t] = np.float32(1.0) / np.float32(cnt)
        m = dict(shared)
        m["xT"] = np.ascontiguousarray(xe.T)
        m["cst"] = cst
        in_maps.append(m)
    return in_maps


_NC_CACHE = {}


def kernel(**inp):
    in_maps = _prep(inp)
    if "nc" not in _NC_CACHE:
        _NC_CACHE["nc"] = build_nc()
    nc = _NC_CACHE["nc"]
    res = run_bass_kernel_spmd(nc, in_maps, core_ids=list(range(8)))
    out = np.empty((2, 8192, D), np.float32)
    for core in range(8):
        b, s = core // 4, core % 4
        out[b, s * TM:(s + 1) * TM, :] = np.asarray(res.results[core]["outT"]).T
    return out
```

```python
import numpy as np
import concourse.bass as bass
import concourse.mybir as mybir
from concourse.bass_utils import run_bass_kernel_spmd

F32 = mybir.dt.float32
BF16 = mybir.dt.bfloat16
AF = mybir.ActivationFunctionType
ALU = mybir.AluOpType

D = 1024
DFF = 2816
NJ = DFF // 128
TM = 2048
TB = 512
EPS = 1e-6
NCST = 176
TABW = 896


class Op:
    __slots__ = ("eng", "fn", "deps", "marked", "sem", "count", "dma")


class Res:
    __slots__ = ("w", "r")

    def __init__(self):
        self.w = None
        self.r = {}


class Sched:
    ENG = ("pe", "act", "dve", "pool", "sp")

    def __init__(self):
        self.q = {e: [] for e in self.ENG}
        self.bar = []
        self.frozen = False

    def op(self, eng, fn, reads=(), writes=(), deps=(), dma=None, nobar=False, force=False):
        o = Op()
        if self.frozen and not force:
            o.eng = eng
            o.dma = dma
            o.marked = False
            o.deps = []
            return o
        o.eng = eng
        o.fn = fn
        o.dma = dma
        o.marked = dma is not None
        o.sem = None
        o.count = 0
        d = []
        for r in reads:
            if r.w is not None:
                d.append(r.w)
        for w in writes:
            if w.w is not None:
                d.append(w.w)
            d.extend(w.r.values())
        d.extend(x for x in deps if x is not None)
        if not nobar and eng != "pe":
            d.extend(self.bar)
        dd = []
        seen = set()
        for x in d:
            if x is o or id(x) in seen:
                continue
            if x.eng == "pe" and eng == "pe" and x.dma is None:
                continue
            seen.add(id(x))
            dd.append(x)
            x.marked = True
        o.deps = dd
        for r in reads:
            r.r[(eng, dma)] = o
        for w in writes:
            w.w = o
            w.r = {}
        self.q[eng].append(o)
        return o

    def barrier(self):
        if self.frozen:
            return
        self.bar = [self.q[e][-1] for e in ("pe", "act", "dve") if self.q[e]]

    def finalize(self):
        keys = set()
        for e in self.ENG:
            n = 0
            cum = {}
            for o in self.q[e]:
                if o.dma is not None:
                    o.sem = o.dma
                    cum[o.dma] = cum.get(o.dma, 0) + 16
                    o.count = cum[o.dma]
                    keys.add(o.dma)
                elif o.marked:
                    n += 1
                    o.sem = "p_" + e
                    o.count = n
                    keys.add(o.sem)
        return sorted(keys)

    def emit(self, ename, eng, sems):
        waited = {}
        for o in self.q[ename]:
            need = {}
            for d in o.deps:
                if need.get(d.sem, 0) < d.count:
                    need[d.sem] = d.count
            for k, c in need.items():
                if waited.get(k, 0) >= c:
                    continue
                eng.wait_ge(sems[k], c)
                waited[k] = c
            ins = o.fn(eng)
            if o.marked:
                ins.then_inc(sems[o.sem], 16 if o.dma is not None else 1)


def build_nc(debug=None, stop=None):
    nc = bass.Bass("TRN2", target_bir_lowering=False)

    def din(name, shape):
        return nc.dram_tensor(name, list(shape), F32, kind="ExternalInput").ap()

    xT = din("xT", [D, TM + TB])
    cst_d = din("cst", [128, NCST])
    w_ada = din("w_ada", [D, 9 * D])
    w1i = din("w1i", [D, 2 * DFF])
    w1o = din("w1o", [DFF, D])
    w3i = din("w3i", [D, 2 * DFF])
    w3o = din("w3o", [DFF, D])
    w_in = din("w_in", [D, 4096])
    tab_d = din("tab", [128, 8 * TABW])
    w_ao = din("w_ao", [512, D])
    w_pg = din("w_pg", [512, 128])
    w_po = din("w_po", [512, D])
    w_o = din("w_o", [D, D])
    outT = nc.dram_tensor("outT", [D, TM], F32, kind="ExternalOutput").ap()
    dbg_d = None
    if debug is not None:
        dbg_d = nc.dram_tensor("dbg", [128, debug[1]], F32, kind="ExternalOutput").ap()

    S = Sched()

    ARENA_BYTES = 207 * 1024
    arena_cm = nc.sbuf_tensor("arena", [128, ARENA_BYTES // 2], BF16)
    psum_cm = nc.psum_tensor("ps", [128, 4096], F32)
    arena = arena_cm.__enter__()
    ps = psum_cm.__enter__()
    off = [0]

    def alloc(nbytes):
        o = off[0]
        off[0] += (nbytes + 31) // 32 * 32
        assert off[0] <= ARENA_BYTES, off[0]
        return o

    def view(o, dt, shape):
        n = int(np.prod(shape))
        if dt == BF16:
            v = arena[:, o // 2: o // 2 + n]
        else:
            v = arena[:, o // 2: o // 2 + 2 * n].bitcast(F32)
        if len(shape) == 2:
            return v.rearrange("p (a b) -> p a b", a=shape[0])
        if len(shape) == 3:
            return v.rearrange("p (a b c) -> p a b c", a=shape[0], b=shape[1])
        return v

    o_xb = alloc(8 * TM * 4)
    XB = view(o_xb, F32, (8, TM))
    o_cst = alloc(NCST * 4)
    CST = view(o_cst, F32, (NCST,))
    o_mod = alloc(72 * 4)
    MOD = view(o_mod, F32, (72,))
    o_der = alloc(9 * 8 * 4)
    DER = view(o_der, F32, (9, 8))
    o_scb = alloc(8 * 2)
    SCB = view(o_scb, BF16, (8,))
    o_ones = alloc(128 * 2)
    ONES = view(o_ones, BF16, (128,))
    o_bones = alloc(128 * 2)
    BONES = view(o_bones, BF16, (128,))
    o_vones = alloc(2 * 64 * 2)
    VONES = view(o_vones, BF16, (2, 64))
    o_epsc = alloc(32)
    EPSC = view(o_epsc, F32, (8,))
    o_wpg = alloc(4 * 128 * 2)
    WPG = view(o_wpg, BF16, (4, 128))
    o_w = alloc(10 * 4096)
    WP = [Res() for _ in range(10)]

    def wpiece(i, shape, npieces=1):
        return view(o_w + 4096 * i, BF16, shape)

    o_uni = alloc(0)
    uni_base = off[0]

    off[0] = uni_base
    o_halo = alloc(8 * TB * 4)
    HALO = view(o_halo, F32, (8, TB))
    o_h = alloc(8 * (TM + TB) * 2)
    H = view(o_h, BF16, (8, TM + TB))
    o_actb = alloc(4 * (TM + TB) * 2)
    ACTB = view(o_actb, BF16, (4, TM + TB))
    RSTD = [view(alloc(TB * 4), F32, (TB,)) for _ in range(2)]
    o_tmp = [alloc(TB * 4) for _ in range(2)]
    TMP = [view(o, F32, (TB,)) for o in o_tmp]
    o_sa = [alloc(TB * 2) for _ in range(2)]
    SA = [view(o, BF16, (TB,)) for o in o_sa]
    o_sqf = alloc(8 * TB * 2)
    SQ_F = view(o_sqf, BF16, (8, TB))
    ACTB4 = view(o_sqf, BF16, (TM + TB,))
    WOX = [view(o_sqf + 5120, BF16, (1024,)), view(alloc(2048), BF16, (1024,))]
    ffn_end = off[0]

    off[0] = uni_base
    o_halo2 = alloc(8 * TB * 4)
    assert o_halo2 == o_halo
    KT = view(o_halo, BF16, (4, 1536))
    MRG = view(o_halo, BF16, (8, 1024))
    o_h2 = alloc(8 * 1536 * 2)
    H2 = view(o_h2, BF16, (8, 1536))
    o_v = alloc(12 * 512 * 2)
    V = view(o_v, BF16, (12, 512))
    o_expb = alloc(8 * TABW * 2)
    EXPB = view(o_expb, BF16, (8, TABW))
    o_reg = alloc(30976)
    TABT = view(o_reg, F32, (8 * TABW,))
    UB = [view(o_reg + 4160 * i, F32, (1040,)) for i in range(2)]
    TT = [view(o_reg + 8320 + 4160 * i, F32, (1040,)) for i in range(2)]
    MIXED = view(o_reg + 16640, BF16, (4, 1024))
    SQK = [view(o_reg + 24832 + 1024 * i, BF16, (TB,)) for i in range(2)]
    RK = [view(o_reg + 26880 + 2048 * i, F32, (TB,)) for i in range(2)]
    ATT = view(o_reg, BF16, (4, 1024))
    PRAW = [view(o_reg + 8192 + 3072 * i, BF16, (1536,)) for i in range(2)]
    PP = [view(o_reg + 14336 + 3072 * i, BF16, (1536,)) for i in range(3)]
    RDEN = [view(o_reg + 23552 + 1024 * i, F32, (256,)) for i in range(2)]
    SG = [view(o_reg + 8192 + 2048 * i, F32, (TB,)) for i in range(4)]
    M1 = [view(o_reg + 16384 + 2048 * i, F32, (TB,)) for i in range(2)]
    QT = view(o_w + 4096 * 6, BF16, (4, 1024))
    YP = view(o_w + 4096 * 8, BF16, (4, 1024))
    SQ_M = view(o_w + 4096 * 6, BF16, (8, TB))
    RSTD_M = [view(o_w + 4096 * 8 + 2048 * i, F32, (TB,)) for i in range(2)]
    TMP_M = [view(o_w + 4096 * 9 + 2048 * i, F32, (TB,)) for i in range(2)]
    mix_end = off[0]
    assert max(ffn_end, mix_end) <= ARENA_BYTES

    BANK = [Res() for _ in range(8)]
    VIEWS = dict(XB=XB, HALO=HALO, H=H, MOD=MOD, DER=DER, CST=CST, ACTB=ACTB, H2=H2, KT=KT, V=V, QT=QT, YP=YP,
                 EXPB=EXPB, ATT=ATT, MRG=MRG, MIXED=MIXED, SCB=SCB)

    def check(name):
        if stop == name:
            S.frozen = True

    def bank(i, n=1):
        return ps[:, i * 512:(i + n) * 512]

    def xblk(b, kt=None):
        if b == 0:
            return HALO[:, :, :] if kt is None else HALO[:, kt, :]
        if kt is None:
            return XB[:, :, (b - 1) * TB:b * TB]
        return XB[:, kt, (b - 1) * TB:b * TB]

    XR = [Res() for _ in range(5)]
    HR = [Res() for _ in range(5)]
    CSTR = Res()
    MODR = Res()
    DERR = Res()
    MISC = Res()

    S.op("sp", lambda e: e.dma_start(out=CST[:, :], in_=cst_d), writes=[CSTR], dma="d_cst")
    xv = xT.rearrange("(k p) t -> p k t", p=128)
    S.op("sp", lambda e: e.dma_start(out=xblk(0), in_=xv[:, :, 0:TB]), writes=[XR[0]], dma="d_x0")

    S.op("dve", lambda e: e.memset(ONES[:, :], 1.0 / 1024.0), writes=[MISC])
    S.op("dve", lambda e: e.memset(BONES[:, :], 0.0), writes=[MISC])
    S.op("dve", lambda e: e.memset(BONES[0:64, 0:64], 1.0 / 64.0), writes=[MISC])
    S.op("dve", lambda e: e.memset(BONES[64:128, 64:128], 1.0 / 64.0), writes=[MISC])
    S.op("dve", lambda e: e.memset(VONES[:, :, :], 1.0), writes=[MISC])
    S.op("dve", lambda e: e.memset(EPSC[:, :], EPS), writes=[MISC])
    S.op("dve", lambda e: e.tensor_scalar(out=VONES[:, 0, :], in0=VONES[:, 0, :], scalar1=CST[:, 110:111],
                                           scalar2=None, op0=ALU.mult), reads=[CSTR, MISC], writes=[MISC])
    S.op("act", lambda e: e.activation(out=SCB[:, :], in_=CST[:, 0:8], func=AF.Silu), reads=[CSTR], writes=[MISC])
    S.op("pool", lambda e: e.dma_start(out=WPG[:, :, :], in_=w_pg.rearrange("(g c) d -> c g d", c=128)),
         writes=[MISC], dma="d_wpg", nobar=True)

    ada_v = w_ada.rearrange("(k p) n -> p k n", p=128)
    MODPS = [bank(7), bank(7)]
    MODBANK = [BANK[7], BANK[7]]

    ADA_DMA = {}

    def ada_chunk(c):
        piece = 4 + (c % 2)
        wt = wpiece(piece, (8, 256))
        ADA_DMA[c] = S.op("pool", lambda e: e.dma_start(out=wt, in_=ada_v[:, :, c * 256:(c + 1) * 256]),
                          writes=[WP[piece]], dma="d_w%d" % piece, nobar=True)
        part = 0 if c < 12 else 1

        def mm(e):
            ins = None
            for jj in range(2):
                j = 2 * c + jj
                for kt in range(8):
                    ins = e.matmul(MODPS[part][:, j:j + 1], lhsT=wt[:, kt, jj * 128:(jj + 1) * 128],
                                   rhs=SCB[:, kt:kt + 1], start=(kt == 0), stop=(kt == 7))
            return ins
        S.op("pe", mm, reads=[WP[piece], MISC], writes=[MODBANK[part]], nobar=True)

    def mod_finish(part, lo, hi, rows):
        S.op("dve", lambda e: e.tensor_tensor(out=MOD[:, lo:hi], in0=MODPS[part][:, lo:hi], in1=CST[:, 8 + lo:8 + hi],
                                               op=ALU.add), reads=[MODBANK[part], CSTR], writes=[MODR], nobar=True)
        for r in rows:
            i, kind = r // 3, r % 3
            if kind == 0:
                gcol = 80 + 8 * i
                S.op("dve", lambda e, i=i, gcol=gcol: e.scalar_tensor_tensor(
                    out=DER[:, 3 * i, :], in0=MOD[:, 24 * i + 8:24 * i + 16], scalar=1.0, in1=CST[:, gcol:gcol + 8],
                    op0=ALU.add, op1=ALU.mult), reads=[MODR, CSTR], writes=[DERR], nobar=True)
            elif kind == 1:
                S.op("dve", lambda e, i=i: e.tensor_copy(out=DER[:, 3 * i + 1, :], in_=MOD[:, 24 * i:24 * i + 8]),
                     reads=[MODR], writes=[DERR], nobar=True)
            else:
                gs = (0.5, 1.0, 0.5)[i]
                S.op("dve", lambda e, i=i, gs=gs: e.tensor_scalar(
                    out=DER[:, 3 * i + 2, :], in0=MOD[:, 24 * i + 16:24 * i + 24], scalar1=gs, scalar2=None,
                    op0=ALU.mult), reads=[MODR], writes=[DERR], nobar=True)

    for c in range(8):
        ada_chunk(c)
    mod_finish(0, 0, 16, [0, 1])
    S.op("sp", lambda e: e.dma_start(out=xblk(1), in_=xv[:, :, TB:2 * TB]),
         writes=[XR[1]], dma="d_x1", deps=[ADA_DMA[5]])
    check("mod0")

    def norm_stats_a(xap_fn, xres, SQb, ssbank, part=None):
        xall = xap_fn(None)
        if part in (None, 0):
            S.op("act", lambda e: e.activation(out=SQb[:, :, :], in_=xall, func=AF.Square),
                 reads=[xres], writes=[NRES["sq"]])
        if part in (2, 3):
            lo = 0 if part == 2 else 4
            S.op("act", lambda e: e.activation(out=SQb[:, lo:lo + 4, :], in_=xall[:, lo:lo + 4, :], func=AF.Square),
                 reads=[xres], writes=[NRES["sq"]])
        if part in (0, 2, 3):
            return

        def mm(e):
            ins = None
            for kt in range(8):
                ins = e.matmul(bank(ssbank), lhsT=ONES[:, :], rhs=SQb[:, kt, :], start=(kt == 0), stop=(kt == 7))
            return ins
        S.op("pe", mm, reads=[NRES["sq"], MISC], writes=[BANK[ssbank]])

    def norm_stats_b(RSTDb, rkey, ssbank):
        S.op("act", lambda e: e.activation(out=RSTDb[:, :], in_=bank(ssbank), func=AF.Ln, bias=EPSC[:, 0:1], scale=1.0),
             reads=[BANK[ssbank], MISC], writes=[NRES[rkey]])
        S.op("act", lambda e: e.activation(out=RSTDb[:, :], in_=RSTDb[:, :], func=AF.Exp, scale=-0.5),
             reads=[NRES[rkey]], writes=[NRES[rkey]])

    def norm_apply(xap_fn, xres, sub, out_fn, hres, RSTDb, rkey, TMPb):
        for kt in range(8):
            tb = TMPb[kt % 2]
            tr = NRES["tmp%d" % (kt % 2)]
            S.op("dve", lambda e, kt=kt, tb=tb: e.scalar_tensor_tensor(
                out=tb[:, :], in0=xap_fn(kt), scalar=DER[:, 3 * sub, kt:kt + 1], in1=RSTDb[:, :],
                op0=ALU.mult, op1=ALU.mult), reads=[xres, NRES[rkey], DERR], writes=[tr])
            if kt % 2 == 0:
                S.op("act", lambda e, kt=kt, tb=tb: e.activation(
                    out=out_fn(kt), in_=tb[:, :], func=AF.Identity, bias=DER[:, 3 * sub + 1, kt:kt + 1], scale=1.0),
                    reads=[tr, DERR], writes=[hres])
            else:
                S.op("dve", lambda e, kt=kt, tb=tb: e.tensor_scalar(
                    out=out_fn(kt), in0=tb[:, :], scalar1=DER[:, 3 * sub + 1, kt:kt + 1], scalar2=None, op0=ALU.add),
                    reads=[tr, DERR], writes=[hres])

    norm_ctr = [0]

    def norm_thunks(items, sub, SQb, RSTDs, TMPb, ssbank, all_dve=False):
        out = []
        for (xf, xr, of, hr) in items:
            i = norm_ctr[0]
            norm_ctr[0] += 1
            rb, rk = RSTDs[i % 2], "rstd%d" % (i % 2)
            th = [lambda xf=xf, xr=xr: norm_stats_a(xf, xr, SQb, ssbank, 0),
                  lambda xf=xf, xr=xr: norm_stats_a(xf, xr, SQb, ssbank, 1),
                  lambda rb=rb, rk=rk: norm_stats_b(rb, rk, ssbank)]
            for kt in range(8):
                th.append(lambda xf=xf, xr=xr, of=of, hr=hr, rb=rb, rk=rk, kt=kt:
                          norm_apply_kt(xf, xr, sub, of, hr, rb, rk, TMPb, kt, all_dve))
            out.append(th)
        return out

    def norm_apply_kt(xap_fn, xres, sub, out_fn, hres, RSTDb, rkey, TMPb, kt, all_dve=False):
        tb = TMPb[kt % 2]
        tr = NRES["tmp%d" % (kt % 2)]
        S.op("dve", lambda e: e.scalar_tensor_tensor(
            out=tb[:, :], in0=xap_fn(kt), scalar=DER[:, 3 * sub, kt:kt + 1], in1=RSTDb[:, :],
            op0=ALU.mult, op1=ALU.mult), reads=[xres, NRES[rkey], DERR], writes=[tr])
        if kt % 2 == 0 and not all_dve:
            S.op("act", lambda e: e.activation(
                out=out_fn(kt), in_=tb[:, :], func=AF.Identity, bias=DER[:, 3 * sub + 1, kt:kt + 1], scale=1.0),
                reads=[tr, DERR], writes=[hres])
        else:
            S.op("dve", lambda e: e.tensor_scalar(
                out=out_fn(kt), in0=tb[:, :], scalar1=DER[:, 3 * sub + 1, kt:kt + 1], scalar2=None, op0=ALU.add),
                reads=[tr, DERR], writes=[hres])

    def norm_seq(items, sub, SQb, RSTDs, TMPb, ssbank, extra=None, all_dve=False):
        ths = norm_thunks(items, sub, SQb, RSTDs, TMPb, ssbank, all_dve)
        extra = list(extra or [])
        n = len(ths)
        if n == 0:
            return
        ths[0][0]()
        ths[0][1]()
        ths[0][2]()
        for k in range(n):
            if k + 1 < n:
                ths[k + 1][0]()
                ths[k + 1][1]()
            for t in ths[k][3:]:
                t()
                if k >= 1 and extra:
                    extra.pop(0)()
            if k + 1 < n:
                ths[k + 1][2]()
        for t in extra:
            t()

    NRES = {k: Res() for k in ("sq", "rstd0", "rstd1", "tmp0", "tmp1")}

    outv = outT.rearrange("(k p) t -> p k t", p=128)

    def ffn(w_i, w_o_, sub, blocks, first):
        wi_v = w_i.rearrange("(k p) n -> p k n", p=128)
        wo_v = w_o_.rearrange("(k p) n -> p k n", p=128)
        pre = {}
        if first:
            wa0 = wpiece(0, (8, 256))
            wb0 = wpiece(1, (8, 256))
            S.op("pool", lambda e: e.dma_start(out=wa0, in_=wi_v[:, :, 0:256]),
                 writes=[WP[0]], dma="d_w0", nobar=True)
            S.op("pool", lambda e: e.dma_start(out=wb0, in_=wi_v[:, :, DFF:DFF + 256]),
                 writes=[WP[1]], dma="d_w1", nobar=True)
            pre[0] = True
            wlast = S.q["pool"][-1]
            for b in range(2, 5):
                S.op("sp", lambda e, b=b: e.dma_start(out=xblk(b), in_=xv[:, :, b * TB:(b + 1) * TB]),
                     writes=[XR[b]], dma="d_x%d" % b, deps=[wlast])

        def nitem(b):
            return (lambda kt, b=b: xblk(b, kt), XR[b], lambda kt, b=b: H[:, kt, b * TB:(b + 1) * TB], HR[b])
        pending_norms = list(blocks)
        norm_seq([nitem(pending_norms.pop(0)), nitem(pending_norms.pop(0))], sub, SQ_F, RSTD, TMP, 6)
        ACTR = {}
        SAR = [Res(), Res()]
        cnt = [0]
        ada_next = [8] if first else [36]
        GSIZES = [4, 4, 4, 5, 5]
        WOXR = [Res(), Res()]
        GSTART = [sum(GSIZES[:g]) for g in range(len(GSIZES))]
        ngroups = len(GSIZES)
        chunk_ctr = [0]

        def wi_load(g, c2):
            j0 = GSTART[g] + 2 * c2
            nt = min(2, GSTART[g] + GSIZES[g] - j0)
            sl = chunk_ctr[0] % 2
            chunk_ctr[0] += 1
            wa = wpiece(2 * sl, (8, 256))
            wb = wpiece(2 * sl + 1, (8, 256))
            if not (g == 0 and c2 == 0 and pre.get(0)):
                S.op("pool", lambda e: e.dma_start(out=wa[:, :, 0:nt * 128], in_=wi_v[:, :, j0 * 128:(j0 + nt) * 128]),
                     writes=[WP[2 * sl]], dma="d_w%d" % (2 * sl), nobar=True)
                S.op("pool", lambda e: e.dma_start(out=wb[:, :, 0:nt * 128],
                                                   in_=wi_v[:, :, DFF + j0 * 128:DFF + (j0 + nt) * 128]),
                     writes=[WP[2 * sl + 1]], dma="d_w%d" % (2 * sl + 1), nobar=True)
            return sl, wa, wb, nt

        def actb(jl, b):
            if jl == 4:
                return ACTB4[:, b * TB:(b + 1) * TB]
            return ACTB[:, jl, b * TB:(b + 1) * TB]

        def unit(g, c2, jj, b, sl, wa, wb):
            jl = 2 * c2 + jj
            i = cnt[0]
            cnt[0] += 1
            ba, bb = i % 2, 2 + i % 2
            hb = H[:, :, b * TB:(b + 1) * TB]

            def mma(e):
                ins = None
                for kt in range(8):
                    ins = e.matmul(bank(ba), lhsT=wa[:, kt, jj * 128:(jj + 1) * 128], rhs=hb[:, kt, :],
                                   start=(kt == 0), stop=(kt == 7))
                return ins

            def mmb(e):
                ins = None
                for kt in range(8):
                    ins = e.matmul(bank(bb), lhsT=wb[:, kt, jj * 128:(jj + 1) * 128], rhs=hb[:, kt, :],
                                   start=(kt == 0), stop=(kt == 7))
                return ins
            S.op("pe", mma, reads=[WP[2 * sl], HR[b]], writes=[BANK[ba]])
            S.op("pe", mmb, reads=[WP[2 * sl + 1], HR[b]], writes=[BANK[bb]])
            S.op("act", lambda e: e.activation(out=SA[i % 2][:, :], in_=bank(ba), func=AF.Silu),
                 reads=[BANK[ba]], writes=[SAR[i % 2]])
            ar = ACTR.setdefault((jl, b), Res())
            S.op("dve", lambda e: e.tensor_tensor(
                out=actb(jl, b), in0=SA[i % 2][:, :], in1=bank(bb), op=ALU.mult),
                reads=[SAR[i % 2], BANK[bb]], writes=[ar],
                deps=([NRES["sq"].w] + list(NRES["sq"].r.values())) if jl == 4 else ())
            if ada_next[0] < 36 and ((ada_next[0] < 12 and i >= 8) or (ada_next[0] >= 12 and i % 3 == 2)):
                ada_chunk(ada_next[0])
                ada_next[0] += 1
                if ada_next[0] == 12:
                    mod_finish(0, 16, 24, [2])
                if ada_next[0] == 36:
                    mod_finish(1, 24, 72, [3, 4, 5, 6, 7, 8])

        for g in range(ngroups):
            nk = GSIZES[g]
            nch = (nk + 1) // 2
            if g == 0:
                chunks = [wi_load(g, c2) for c2 in range(nch)]
                nth_ = []
                carry, carry_next = [], []
                for b in blocks:
                    u = 0
                    for t in carry:
                        t()
                    carry, carry_next = carry_next, []
                    for c2 in range(nch):
                        sl_, wa_, wb_, nt_ = chunks[c2]
                        for jj in range(nt_):
                            unit(g, c2, jj, b, sl_, wa_, wb_)
                            u += 1
                            if u == 2 and pending_norms:
                                cur_item = nitem(pending_norms.pop(0))
                                nth_ = norm_thunks([cur_item], sub, SQ_F, RSTD, TMP, 6, False)[0]
                                norm_stats_a(cur_item[0], cur_item[1], SQ_F, 6, 2)
                            elif u == 3 and nth_:
                                norm_stats_a(cur_item[0], cur_item[1], SQ_F, 6, 3)
                            elif u == 4 and nth_:
                                nth_[1]()
                                nth_[2]()
                                carry_next = nth_[3:]
                                nth_ = []
                            for t in carry[:2]:
                                t()
                            carry = carry[2:]
                for t in carry + carry_next:
                    t()
            else:
                for c2 in range(nch):
                    sl_, wa_, wb_, nt_ = wi_load(g, c2)
                    for jj in range(nt_):
                        for b in blocks:
                            unit(g, c2, jj, b, sl_, wa_, wb_)
            so = g % 2
            wo = view(o_w + 4096 * (6 + 2 * so), BF16, (4, 1024))
            nk4 = min(nk, 4)
            S.op("pool", lambda e, wo=wo, g=g, nk4=nk4: e.dma_start(out=wo[:, 0:nk4, :],
                                                                    in_=wo_v[:, GSTART[g]:GSTART[g] + nk4, :]),
                 writes=[WP[6 + 2 * so], WP[7 + 2 * so]], dma="d_w%d" % (6 + 2 * so))
            wox = WOX[so]
            if nk == 5:
                S.op("pool", lambda e, wox=wox, g=g: e.dma_start(out=wox[:, :], in_=wo_v[:, GSTART[g] + 4, :]),
                     writes=[WOXR[so]], dma="d_wox%d" % so,
                     deps=[NRES["sq"].w] + list(NRES["sq"].r.values()))
            last = (g == ngroups - 1)
            order = ([(b, n) for b in blocks for n in range(0, 8, 2)] if last
                     else [(b, n) for n in range(0, 8, 2) for b in blocks])
            k = 0
            for (b, n) in order:
                ob = (4, 0, 2)[k % 3]
                k += 1

                def mmo(e, wo=wo, wox=wox, n=n, b=b, ob=ob, nk=nk):
                    ins = None
                    for dn in range(2):
                        cs = slice((n + dn) * 128, (n + dn + 1) * 128)
                        for jl in range(nk):
                            lw = wox[:, cs] if jl == 4 else wo[:, jl, cs]
                            ins = e.matmul(bank(ob + dn), lhsT=lw, rhs=actb(jl, b),
                                           start=(jl == 0), stop=(jl == nk - 1))
                    return ins
                S.op("pe", mmo, reads=[WP[6 + 2 * so], WP[7 + 2 * so]] + ([WOXR[so]] if nk == 5 else [])
                     + [ACTR[(jl, b)] for jl in range(nk)], writes=[BANK[ob], BANK[ob + 1]])
                evs = []
                for dn in range(2):
                    evs.append(S.op("dve", lambda e, n=n + dn, b=b, obb=ob + dn: e.scalar_tensor_tensor(
                        out=xblk(b, n), in0=bank(obb), scalar=DER[:, 3 * sub + 2, n:n + 1], in1=xblk(b, n),
                        op0=ALU.mult, op1=ALU.add), reads=[BANK[ob + dn], DERR], writes=[XR[b]]))
                if last and not first:
                    S.op("sp", lambda e, n=n, b=b: e.dma_start(
                        out=outv[:, n:n + 2, (b - 1) * TB:b * TB], in_=XB[:, n:n + 2, (b - 1) * TB:b * TB]),
                        deps=evs, dma="d_st", force=True)

    ffn(w1i, w1o, 0, [0, 1, 2, 3, 4], True)
    check("ffn1")
    S.barrier()

    TABR = Res()
    S.op("sp", lambda e: e.dma_start(out=TABT[:, :], in_=tab_d), writes=[TABR], dma="d_tab")
    EXPR = Res()

    win_v = w_in.rearrange("(k p) n -> p k n", p=128)
    ring = [0]
    RING = [0, 1, 2, 3, 4, 5]

    def load_w(src_ap, shape):
        piece = RING[ring[0] % 6]
        ring[0] += 1
        wt = wpiece(piece, shape)
        S.op("pool", lambda e: e.dma_start(out=wt, in_=src_ap), writes=[WP[piece]], dma="d_w%d" % piece, nobar=True)
        return wt, WP[piece]

    wao_v = w_ao.rearrange("(k p) n -> p k n", p=128)
    wpo_v = w_po.rearrange("(k p) n -> p k n", p=128)
    wo_v2 = w_o.rearrange("(k p) n -> p k n", p=128)
    expb_pstride = EXPB[:, 0, :].ap[0][0]

    for pas in (0, 1):
        eblk = [2 * pas + e for e in range(3)]
        if pas == 0:
            items = []
            for e_i, b in enumerate(eblk):
                items.append((lambda kt, b=b: xblk(b, kt), XR[b],
                              lambda kt, e_i=e_i: H2[:, kt, e_i * TB:(e_i + 1) * TB], HR[e_i]))
            tab_th = [lambda h=h: S.op("act", lambda e: e.activation(
                out=EXPB[:, h, :], in_=TABT[:, h * TABW:(h + 1) * TABW], func=AF.Exp),
                reads=[TABR], writes=[EXPR]) for h in range(8)]
            norm_seq(items, 1, SQ_M, RSTD_M, TMP_M, 6, extra=tab_th)
        check("s1_%d" % pas)
        S.barrier()
        KR = [Res() for _ in range(3)]
        QR = Res()
        VR = [Res() for _ in range(12)]
        SQKR = [Res(), Res()]
        RKR = [Res(), Res()]
        cnt = 0
        UR = [Res(), Res()]
        TR = [Res(), Res()]
        MXR = [Res() for _ in range(4)]
        YPR = Res()
        uw = [load_w(win_v[:, :, 1536 + c * 256:1536 + (c + 1) * 256], (8, 256)) for c in range(2)]

        def u_proj(gq):
            wt, wr = uw[gq // 2]
            f = gq % 2
            ub = UB[gq % 2]
            ur = UR[gq % 2]
            for e_i in range(3):
                ubk = 6 + (3 * gq + e_i) % 2
                if e_i == 0:
                    rhs_fn = lambda kt: H2[:, kt, TB - 64:TB]
                    ncol = 64
                else:
                    rhs_fn = lambda kt, e_i=e_i: H2[:, kt, e_i * TB:(e_i + 1) * TB]
                    ncol = TB

                def mmu(e, rhs_fn=rhs_fn, ubk=ubk, ncol=ncol):
                    ins = None
                    for kt in range(8):
                        ins = e.matmul(bank(ubk)[:, 0:ncol], lhsT=wt[:, kt, f * 128:(f + 1) * 128], rhs=rhs_fn(kt),
                                       start=(kt == 0), stop=(kt == 7))
                    return ins
                S.op("pe", mmu, reads=[wr, HR[e_i]], writes=[BANK[ubk]])
                if e_i == 0:
                    if pas == 0:
                        S.op("dve", lambda e, ubk=ubk: e.tensor_scalar(
                            out=ub[:, 0:16], in0=bank(ubk)[:, 48:64], scalar1=CST[:, 110:111], scalar2=None,
                            op0=ALU.mult), reads=[BANK[ubk], CSTR], writes=[ur])
                    else:
                        S.op("dve", lambda e, ubk=ubk: e.tensor_copy(out=ub[:, 0:16], in_=bank(ubk)[:, 48:64]),
                             reads=[BANK[ubk]], writes=[ur])
                else:
                    S.op("act", lambda e, ubk=ubk, e_i=e_i: e.activation(
                        out=ub[:, 16 + (e_i - 1) * TB:16 + e_i * TB], in_=bank(ubk), func=AF.Copy),
                        reads=[BANK[ubk]], writes=[ur])

        def u_pool(gq):
            ub = UB[gq % 2]
            ur = UR[gq % 2]
            th = []
            nsteps = gq + 1
            src, sres = ub, ur
            for st in range(nsteps):
                sh = 1 << st
                dst, dres = TT[st % 2], TR[st % 2]
                lo = 2 * sh - 1
                th.append(lambda src=src, dst=dst, sh=sh, lo=lo, sres=sres, dres=dres: S.op(
                    "dve", lambda e: e.tensor_tensor(
                        out=dst[:, lo:1040], in0=src[:, lo:1040], in1=src[:, lo - sh:1040 - sh], op=ALU.add),
                    reads=[sres], writes=[dres]))
                src, sres = dst, dres
            w = 2 << gq
            th.append(lambda src=src, sres=sres: S.op("dve", lambda e: e.scalar_tensor_tensor(
                out=MIXED[:, gq, :], in0=src[:, 16:1040], scalar=1.0 / w, in1=ub[:, 16:1040],
                op0=ALU.mult, op1=ALU.subtract), reads=[sres, ur], writes=[MXR[gq]]))
            if pas == 0:
                t16 = TT[(nsteps) % 2]
                t16r = TR[(nsteps) % 2]
                th.append(lambda src=src, sres=sres: S.op("dve", lambda e: e.tensor_tensor(
                    out=t16[:, 0:16], in0=src[:, 16:32], in1=CST[:, 112 + 16 * gq:128 + 16 * gq], op=ALU.mult),
                    reads=[sres, CSTR], writes=[t16r]))
                th.append(lambda: S.op("dve", lambda e: e.tensor_tensor(
                    out=MIXED[:, gq, 0:16], in0=t16[:, 0:16], in1=ub[:, 16:32], op=ALU.subtract),
                    reads=[t16r, ur], writes=[MXR[gq]]))
            return th

        def u_yp(gq):
            for mb in range(2):
                ypb = 4 + mb
                S.op("pe", lambda e, mb=mb, ypb=ypb: e.matmul(
                    bank(ypb), lhsT=WPG[:, gq, :], rhs=MIXED[:, gq, mb * TB:(mb + 1) * TB], start=True, stop=True),
                    reads=[MXR[gq], MISC], writes=[BANK[ypb]])
                S.op("dve", lambda e, mb=mb, ypb=ypb: e.tensor_scalar(
                    out=YP[:, gq, mb * TB:(mb + 1) * TB], in0=bank(ypb), scalar1=CST[:, 106 + gq:107 + gq],
                    scalar2=None, op0=ALU.mult), reads=[BANK[ypb], CSTR], writes=[YPR])

        u_proj(0)
        u_proj(1)
        pool_q = u_pool(0) + u_pool(1)
        units = []
        for which, col0, eb_list, gcol in (("k", 512, [0, 1, 2], 105), ("q", 0, [1, 2], 104)):
            for c in range(2):
                wt, wr = load_w(win_v[:, :, col0 + c * 256:col0 + (c + 1) * 256], (8, 256))
                for f in range(2):
                    for e_i in eb_list:
                        units.append((which, wt, wr, f, 2 * c + f, e_i, gcol))

        def kq_raw(i):
            which, wt, wr, f, ft, e_i, gcol = units[i]
            pb = i % 4
            hb = H2[:, :, e_i * TB:(e_i + 1) * TB]

            def mmk(e):
                ins = None
                for kt in range(8):
                    ins = e.matmul(bank(pb), lhsT=wt[:, kt, f * 128:(f + 1) * 128], rhs=hb[:, kt, :],
                                   start=(kt == 0), stop=(kt == 7))
                return ins
            S.op("pe", mmk, reads=[wr, HR[e_i]], writes=[BANK[pb]])
            S.op("act", lambda e: e.activation(out=SQK[i % 2][:, :], in_=bank(pb), func=AF.Square),
                 reads=[BANK[pb]], writes=[SQKR[i % 2]])

        def kq_rest(i):
            which, wt, wr, f, ft, e_i, gcol = units[i]
            pb = i % 4
            sb_ = 4 + i % 2
            S.op("pe", lambda e: e.matmul(bank(sb_), lhsT=BONES[:, :], rhs=SQK[i % 2][:, :], start=True, stop=True),
                 reads=[SQKR[i % 2], MISC], writes=[BANK[sb_]])
            S.op("act", lambda e: e.activation(out=RK[i % 2][:, :], in_=bank(sb_), func=AF.Ln, bias=EPSC[:, 0:1],
                                               scale=1.0), reads=[BANK[sb_], MISC], writes=[RKR[i % 2]])
            S.op("act", lambda e: e.activation(out=RK[i % 2][:, :], in_=RK[i % 2][:, :], func=AF.Exp, scale=-0.5),
                 reads=[RKR[i % 2]], writes=[RKR[i % 2]])
            if which == "k":
                dst = KT[:, ft, e_i * TB:(e_i + 1) * TB]
                dres = KR[e_i]
            else:
                dst = QT[:, ft, (e_i - 1) * TB:e_i * TB]
                dres = QR
            S.op("dve", lambda e: e.scalar_tensor_tensor(
                out=dst, in0=bank(pb), scalar=CST[:, gcol:gcol + 1], in1=RK[i % 2][:, :],
                op0=ALU.mult, op1=ALU.mult), reads=[BANK[pb], RKR[i % 2], CSTR], writes=[dres])

        kq_raw(0)
        for i in range(len(units)):
            if i + 1 < len(units):
                kq_raw(i + 1)
            kq_rest(i)
            if pool_q:
                pool_q.pop(0)()
            if i == 9:
                while pool_q:
                    pool_q.pop(0)()
                u_proj(2)
                u_proj(3)
                pool_q = u_pool(2) + u_pool(3)
        while pool_q:
            pool_q.pop(0)()
        wv0, wvr0 = load_w(win_v[:, :, 1024:1280], (8, 256))
        wv1, wvr1 = load_w(win_v[:, :, 1280:1536], (8, 256))
        for t in range(12):
            vb = 4 + t % 2

            def mmv(e, t=t, vb=vb, wv0=wv0, wv1=wv1):
                ins = None
                for half, wv in ((0, wv0), (1, wv1)):
                    for kt in range(8):
                        ins = e.matmul(bank(vb)[:, half * 256:(half + 1) * 256], lhsT=H2[:, kt, t * 128:(t + 1) * 128],
                                       rhs=wv[:, kt, :], start=(kt == 0), stop=(kt == 7))
                return ins
            S.op("pe", mmv, reads=[wvr0, wvr1, HR[t // 4]], writes=[BANK[vb]])
            if pas == 0 and t < 4:
                S.op("dve", lambda e, t=t, vb=vb: e.tensor_scalar(out=V[:, t, :], in0=bank(vb), scalar1=CST[:, 110:111],
                                                                  scalar2=None, op0=ALU.mult),
                     reads=[BANK[vb], CSTR], writes=[VR[t]])
            else:
                S.op("act", lambda e, t=t, vb=vb: e.activation(out=V[:, t, :], in_=bank(vb), func=AF.Copy),
                     reads=[BANK[vb]], writes=[VR[t]])
        for gq in range(4):
            u_yp(gq)
        check("s2_%d" % pas)
        S.barrier()
        PRR = [Res(), Res()]
        PPR = [Res(), Res(), Res()]
        RDR = [Res(), Res()]
        ATTR = Res()
        def att_front(u):
            qb, h = u // 8, u % 8
            s = u % 2
            ft, po = h // 2, (h % 2) * 64
            sb0 = 3 * s

            def mms(e):
                ins = None
                for jj in range(6):
                    kt_ = 2 * qb + 5 - jj
                    ins = e.matmul(ps[:, sb0 * 512 + jj * 256:sb0 * 512 + (jj + 1) * 256],
                                   lhsT=KT[po:po + 64, ft, kt_ * 128:(kt_ + 1) * 128],
                                   rhs=QT[po:po + 64, ft, qb * 256:(qb + 1) * 256], start=True, stop=True)
                return ins
            S.op("pe", mms, reads=KR + [QR], writes=[BANK[sb0], BANK[sb0 + 1], BANK[sb0 + 2]])
            S.op("act", lambda e: e.activation(out=PRAW[s][:, :], in_=bank(sb0, 3), func=AF.Exp, scale=0.125),
                 reads=[BANK[sb0], BANK[sb0 + 1], BANK[sb0 + 2]], writes=[PRR[s]])
            ewin = bass.AP(EXPB.tensor, EXPB[:, h, 0:1].offset, [[expb_pstride, 128], [128, 6], [1, 256]])
            S.op("dve", lambda e: e.tensor_tensor(
                out=PP[u % 3][:, :].rearrange("p (a b) -> p a b", a=6),
                in0=PRAW[s][:, :].rearrange("p (a b) -> p a b", a=6), in1=ewin, op=ALU.mult),
                reads=[PRR[s], EXPR], writes=[PPR[u % 3]])

        def att_back(u, pas=pas):
            qb, h = u // 8, u % 8
            s = u % 2
            ft, po = h // 2, (h % 2) * 64
            pr = (u // 2) % 2
            ndb = 6 + pr

            def mmpv(e):
                ins = None
                for jj in range(6):
                    kt_ = 2 * qb + 5 - jj
                    ins = e.matmul(bank(ndb)[po:po + 64, 0:256], lhsT=V[:, kt_, h * 64:(h + 1) * 64],
                                   rhs=PP[u % 3][:, jj * 256:(jj + 1) * 256], start=(jj == 0), stop=(jj == 5))
                for jj in range(6):
                    kt_ = 2 * qb + 5 - jj
                    var = 0 if (pas == 0 and kt_ < 4) else 1
                    ins = e.matmul(bank(ndb)[po:po + 64, 256:512], lhsT=VONES[:, var, :],
                                   rhs=PP[u % 3][:, jj * 256:(jj + 1) * 256], start=(jj == 0), stop=(jj == 5))
                return ins
            S.op("pe", mmpv, reads=VR + [PPR[u % 3], MISC], writes=[BANK[ndb]])
            if h % 2 == 1:
                S.op("act", lambda e: e.activation(out=RDEN[pr][:, :], in_=bank(ndb)[:, 256:512], func=AF.Ln),
                     reads=[BANK[ndb]], writes=[RDR[pr]])
                S.op("act", lambda e: e.activation(out=RDEN[pr][:, :], in_=RDEN[pr][:, :], func=AF.Exp, scale=-1.0),
                     reads=[RDR[pr]], writes=[RDR[pr]])
                S.op("dve", lambda e: e.tensor_tensor(
                    out=ATT[:, ft, qb * 256:(qb + 1) * 256], in0=bank(ndb)[:, 0:256],
                    in1=RDEN[pr][:, :], op=ALU.mult), reads=[BANK[ndb], RDR[pr]], writes=[ATTR])

        att_front(0)
        att_front(1)
        for u in range(32):
            if u + 2 < 32:
                att_front(u + 2)
            att_back(u)
        check("s3_%d" % pas)
        S.barrier()
        SGR = [Res() for _ in range(4)]
        M1R = [Res(), Res()]
        MRGR = Res()
        it = 0
        for c in range(4):
            wga, rga = load_w(win_v[:, :, 2048 + c * 256:2048 + (c + 1) * 256], (8, 256))
            wgb, rgb = load_w(win_v[:, :, 3072 + c * 256:3072 + (c + 1) * 256], (8, 256))
            wao, rao = load_w(wao_v[:, :, c * 256:(c + 1) * 256], (4, 256))
            wpo, rpo = load_w(wpo_v[:, :, c * 256:(c + 1) * 256], (4, 256))
            for f in range(2):
                n = 2 * c + f
                for mb in range(2):
                    s = it % 2
                    it += 1
                    b0 = 4 * s
                    hb = H2[:, :, (1 + mb) * TB:(2 + mb) * TB]
                    fs = slice(f * 128, (f + 1) * 128)
                    ms = slice(mb * TB, (mb + 1) * TB)

                    def mm_ya(e, wao=wao, fs=fs, ms=ms, b0=b0):
                        ins = None
                        for hp in range(4):
                            ins = e.matmul(bank(b0), lhsT=wao[:, hp, fs], rhs=ATT[:, hp, ms], start=(hp == 0), stop=(hp == 3))
                        return ins

                    def mm_yb(e, wpo=wpo, fs=fs, ms=ms, b0=b0):
                        ins = None
                        for g in range(4):
                            ins = e.matmul(bank(b0 + 1), lhsT=wpo[:, g, fs], rhs=YP[:, g, ms], start=(g == 0), stop=(g == 3))
                        return ins

                    def mm_ga(e, wga=wga, fs=fs, hb=hb, b0=b0):
                        ins = None
                        for kt in range(8):
                            ins = e.matmul(bank(b0 + 2), lhsT=wga[:, kt, fs], rhs=hb[:, kt, :], start=(kt == 0), stop=(kt == 7))
                        return ins

                    def mm_gb(e, wgb=wgb, fs=fs, hb=hb, b0=b0):
                        ins = None
                        for kt in range(8):
                            ins = e.matmul(bank(b0 + 3), lhsT=wgb[:, kt, fs], rhs=hb[:, kt, :], start=(kt == 0), stop=(kt == 7))
                        return ins
                    S.op("pe", mm_ga, reads=[rga, HR[1 + mb]], writes=[BANK[b0 + 2]])
                    S.op("pe", mm_gb, reads=[rgb, HR[1 + mb]], writes=[BANK[b0 + 3]])
                    S.op("pe", mm_ya, reads=[rao, ATTR], writes=[BANK[b0]])
                    S.op("pe", mm_yb, reads=[rpo, YPR], writes=[BANK[b0 + 1]])
                    S.op("act", lambda e, s=s, b0=b0: e.activation(out=SG[2 * s][:, :], in_=bank(b0 + 2), func=AF.Sigmoid),
                         reads=[BANK[b0 + 2]], writes=[SGR[2 * s]])
                    S.op("act", lambda e, s=s, b0=b0: e.activation(out=SG[2 * s + 1][:, :], in_=bank(b0 + 3), func=AF.Sigmoid),
                         reads=[BANK[b0 + 3]], writes=[SGR[2 * s + 1]])
                    S.op("dve", lambda e, s=s, b0=b0: e.tensor_tensor(out=M1[s][:, :], in0=bank(b0), in1=SG[2 * s][:, :],
                                                                      op=ALU.mult),
                         reads=[BANK[b0], SGR[2 * s]], writes=[M1R[s]])
                    S.op("dve", lambda e, s=s, b0=b0: e.tensor_tensor(out=SG[2 * s + 1][:, :], in0=bank(b0 + 1),
                                                                      in1=SG[2 * s + 1][:, :], op=ALU.mult),
                         reads=[BANK[b0 + 1], SGR[2 * s + 1]], writes=[SGR[2 * s + 1]])
                    S.op("dve", lambda e, s=s, n=n, ms=ms: e.tensor_tensor(out=MRG[:, n, ms], in0=M1[s][:, :],
                                                                           in1=SG[2 * s + 1][:, :], op=ALU.add),
                         reads=[M1R[s], SGR[2 * s + 1]], writes=[MRGR])
        check("s4_%d" % pas)
        S.barrier()
        side = []
        if pas == 0:
            side.append(lambda: S.op("dve", lambda e: e.tensor_copy(out=H2[:, :, 0:TB], in_=H2[:, :, 2 * TB:3 * TB]),
                                     reads=[HR[2]], writes=[HR[0]]))
            its = [(lambda kt, b=b: xblk(b, kt), XR[b],
                    lambda kt, e_i=e_i: H2[:, kt, e_i * TB:(e_i + 1) * TB], HR[e_i]) for e_i, b in ((1, 3), (2, 4))]
            nt = norm_thunks(its, 1, SQ_M, RSTD_M, TMP_M, 6)
            side += nt[0][:3] + nt[1][:2] + nt[0][3:] + [nt[1][2]] + nt[1][3:]
        it = 0
        for c in range(4):
            wo_, ro_ = load_w(wo_v2[:, :, c * 256:(c + 1) * 256], (8, 256))
            for f in range(2):
                n = 2 * c + f
                for mb in range(2):
                    ob = it % 2
                    it += 1
                    b = 2 * pas + 1 + mb
                    ms = slice(mb * TB, (mb + 1) * TB)

                    def mm_o(e, wo_=wo_, f=f, ms=ms, ob=ob):
                        ins = None
                        for kt in range(8):
                            ins = e.matmul(bank(ob), lhsT=wo_[:, kt, f * 128:(f + 1) * 128], rhs=MRG[:, kt, ms],
                                           start=(kt == 0), stop=(kt == 7))
                        return ins
                    S.op("pe", mm_o, reads=[ro_, MRGR], writes=[BANK[ob]])
                    S.op("dve", lambda e, n=n, b=b, ob=ob: e.scalar_tensor_tensor(
                        out=xblk(b, n), in0=bank(ob), scalar=DER[:, 5, n:n + 1], in1=xblk(b, n),
                        op0=ALU.mult, op1=ALU.add), reads=[BANK[ob], DERR], writes=[XR[b]])
                    for t in side[:2]:
                        t()
                    side = side[2:]
        for t in side:
            t()
        check("s5_%d" % pas)
        S.barrier()

    ffn(w3i, w3o, 2, [1, 2, 3, 4], False)
    STR = Res()
    if S.frozen:
        for b in range(1, 5):
            S.op("sp", lambda e, b=b: e.dma_start(out=outv[:, :, (b - 1) * TB:b * TB], in_=xblk(b)),
                 reads=[XR[b]], writes=[STR], dma="d_st", force=True)
    if dbg_d is not None:
        S.op("pool", lambda e: e.dma_start(out=(dbg_d if len(debug[0](VIEWS).shape) == 2 else dbg_d.rearrange("p (a b) -> p a b", a=debug[0](VIEWS).shape[1])), in_=debug[0](VIEWS)), reads=[], writes=[STR], dma="d_dbg",
             deps=[S.q[x][-1] for x in ("pe", "act", "dve") if S.q[x]], force=True)
        dbg_op = S.q["pool"][-1]
    final = S.q["sp"][-1]

    keys = S.finalize()
    total_st = final.count
    sem_cms = {k: nc.semaphore(k) for k in keys}
    sems = {k: cm.__enter__() for k, cm in sem_cms.items()}
    with nc.Block() as block:
        @block.sync
        def _(e):
            S.emit("sp", e, sems)
            e.wait_ge(sems["d_st"], total_st)
            if dbg_d is not None:
                e.wait_ge(sems["d_dbg"], 16)

        @block.gpsimd
        def _(e):
            S.emit("pool", e, sems)

        @block.tensor
        def _(e):
            S.emit("pe", e, sems)

        @block.scalar
        def _(e):
            S.emit("act", e, sems)

        @block.vector
        def _(e):
            S.emit("dve", e, sems)
    for cm in sem_cms.values():
        cm.__exit__(None, None, None)
    psum_cm.__exit__(None, None, None)
    arena_cm.__exit__(None, None, None)
    return nc


def _prep(inp):
    f = lambda k: np.asarray(inp[k], dtype=np.float32)
    x = f("x")
    c = f("c")
    rel_bias = f("rel_bias")[0]
    k = np.arange(128)[:, None]
    i = np.arange(TABW)[None, :]
    d = i - 128 - k
    cd = i // 64 - 2 - k // 64
    valid = (cd >= 0) & (cd <= 8)
    idx = np.clip(d, -128, 128) + 128
    tab = np.empty((128, 8, TABW), np.float32)
    for h in range(8):
        tab[:, h, :] = np.where(valid, rel_bias[h][idx], np.float32(-30000.0))
    tab = np.ascontiguousarray(tab.reshape(128, 8 * TABW))
    shared = {
        "w_ada": np.ascontiguousarray(f("w_ada")[0]),
        "w1i": np.ascontiguousarray(f("w_ffn1_in")[0]),
        "w1o": np.ascontiguousarray(f("w_ffn1_out")[0]),
        "w3i": np.ascontiguousarray(f("w_ffn2_in")[0]),
        "w3o": np.ascontiguousarray(f("w_ffn2_out")[0]),
        "w_in": np.ascontiguousarray(f("w_in")[0]),
        "tab": tab,
        "w_ao": np.ascontiguousarray(f("w_attn_out")[0]),
        "w_pg": np.ascontiguousarray(f("w_pool_group")[0].reshape(512, 128)),
        "w_po": np.ascontiguousarray(f("w_pool_out")[0]),
        "w_o": np.ascontiguousarray(f("w_o")[0]),
    }
    col = lambda v: v.reshape(-1, 128).T
    in_maps = []
    wins = (2, 4, 8, 16)
    for core in range(8):
        b, s = core // 4, core % 4
        xe = np.zeros((TM + TB, D), np.float32)
        if s == 0:
            xe[TB:] = x[b, 0:TM]
        else:
            xe[:] = x[b, s * TM - TB:(s + 1) * TM]
        cst = np.zeros((128, NCST), np.float32)
        cst[:, 0:8] = col(c[b])
        cst[:, 8:80] = col(f("b_ada")[0])
        cst[:, 80:88] = col(f("g_ffn1")[0])
        cst[:, 88:96] = col(f("g_mix")[0])
        cst[:, 96:104] = col(f("g_ffn2")[0])
        cst[:, 104] = np.tile(f("q_gain")[0], 2)
        cst[:, 105] = np.tile(f("k_gain")[0], 2)
        cst[:, 106:110] = col(f("pool_scale")[0])
        cst[:, 110] = 0.0 if s == 0 else 1.0
        cst[:, 111] = 1.0
        for g in range(4):
            for t in range(16):
                cnt = min(t + 1, wins[g]) if s == 0 else wins[g]
                cst[:, 112 + 16 * g + t] = np.float32(1.0) / np.float32(cnt)
        m = dict(shared)
        m["xT"] = np.ascontiguousarray(xe.T)
        m["cst"] = cst
        in_maps.append(m)
    return in_maps


_NC_CACHE = {}


def kernel(**inp):
    in_maps = _prep(inp)
    if "nc" not in _NC_CACHE:
        _NC_CACHE["nc"] = build_nc()
    nc = _NC_CACHE["nc"]
    res = run_bass_kernel_spmd(nc, in_maps, core_ids=list(range(8)))
    out = np.empty((2, 8192, D), np.float32)
    for core in range(8):
        b, s = core // 4, core % 4
        out[b, s * TM:(s + 1) * TM, :] = np.asarray(res.results[core]["outT"]).T
    return out
```

```python
import numpy as np
import concourse.bass as bass
import concourse.mybir as mybir
from concourse.bass_utils import run_bass_kernel_spmd

F32 = mybir.dt.float32
BF16 = mybir.dt.bfloat16
AF = mybir.ActivationFunctionType
ALU = mybir.AluOpType

D = 1024
DFF = 2816
NJ = DFF // 128
TM = 2048
TB = 512
EPS = 1e-6
NCST = 176
TABW = 896


class Op:
    __slots__ = ("eng", "fn", "deps", "marked", "sem", "count", "dma")


class Res:
    __slots__ = ("w", "r")

    def __init__(self):
        self.w = None
        self.r = {}


class Sched:
    ENG = ("pe", "act", "dve", "pool", "sp")

    def __init__(self):
        self.q = {e: [] for e in self.ENG}
        self.bar = []
        self.frozen = False

    def op(self, eng, fn, reads=(), writes=(), deps=(), dma=None, nobar=False, force=False):
        o = Op()
        if self.frozen and not force:
            o.eng = eng
            o.dma = dma
            o.marked = False
            o.deps = []
            return o
        o.eng = eng
        o.fn = fn
        o.dma = dma
        o.marked = dma is not None
        o.sem = None
        o.count = 0
        d = []
        for r in reads:
            if r.w is not None:
                d.append(r.w)
        for w in writes:
            if w.w is not None:
                d.append(w.w)
            d.extend(w.r.values())
        d.extend(x for x in deps if x is not None)
        if not nobar and eng != "pe":
            d.extend(self.bar)
        dd = []
        seen = set()
        for x in d:
            if x is o or id(x) in seen:
                continue
            if x.eng == "pe" and eng == "pe" and x.dma is None:
                continue
            seen.add(id(x))
            dd.append(x)
            x.marked = True
        o.deps = dd
        for r in reads:
            r.r[(eng, dma)] = o
        for w in writes:
            w.w = o
            w.r = {}
        self.q[eng].append(o)
        return o

    def barrier(self):
        if self.frozen:
            return
        self.bar = [self.q[e][-1] for e in ("pe", "act", "dve") if self.q[e]]

    def finalize(self):
        keys = set()
        for e in self.ENG:
            n = 0
            cum = {}
            for o in self.q[e]:
                if o.dma is not None:
                    o.sem = o.dma
                    cum[o.dma] = cum.get(o.dma, 0) + 16
                    o.count = cum[o.dma]
                    keys.add(o.dma)
                elif o.marked:
                    n += 1
                    o.sem = "p_" + e
                    o.count = n
                    keys.add(o.sem)
        return sorted(keys)

    def emit(self, ename, eng, sems):
        waited = {}
        for o in self.q[ename]:
            need = {}
            for d in o.deps:
                if need.get(d.sem, 0) < d.count:
                    need[d.sem] = d.count
            for k, c in need.items():
                if waited.get(k, 0) >= c:
                    continue
                eng.wait_ge(sems[k], c)
                waited[k] = c
            ins = o.fn(eng)
            if o.marked:
                ins.then_inc(sems[o.sem], 16 if o.dma is not None else 1)


def build_nc(debug=None, stop=None):
    nc = bass.Bass("TRN2", target_bir_lowering=False)

    def din(name, shape):
        return nc.dram_tensor(name, list(shape), F32, kind="ExternalInput").ap()

    xT = din("xT", [D, TM + TB])
    cst_d = din("cst", [128, NCST])
    w_ada = din("w_ada", [D, 9 * D])
    w1i = din("w1i", [D, 2 * DFF])
    w1o = din("w1o", [DFF, D])
    w3i = din("w3i", [D, 2 * DFF])
    w3o = din("w3o", [DFF, D])
    w_in = din("w_in", [D, 4096])
    tab_d = din("tab", [128, 8 * TABW])
    w_ao = din("w_ao", [512, D])
    w_pg = din("w_pg", [512, 128])
    w_po = din("w_po", [512, D])
    w_o = din("w_o", [D, D])
    outT = nc.dram_tensor("outT", [D, TM], F32, kind="ExternalOutput").ap()
    dbg_d = None
    if debug is not None:
        dbg_d = nc.dram_tensor("dbg", [128, debug[1]], F32, kind="ExternalOutput").ap()

    S = Sched()

    ARENA_BYTES = 207 * 1024
    arena_cm = nc.sbuf_tensor("arena", [128, ARENA_BYTES // 2], BF16)
    psum_cm = nc.psum_tensor("ps", [128, 4096], F32)
    arena = arena_cm.__enter__()
    ps = psum_cm.__enter__()
    off = [0]

    def alloc(nbytes):
        o = off[0]
        off[0] += (nbytes + 31) // 32 * 32
        assert off[0] <= ARENA_BYTES, off[0]
        return o

    def view(o, dt, shape):
        n = int(np.prod(shape))
        if dt == BF16:
            v = arena[:, o // 2: o // 2 + n]
        else:
            v = arena[:, o // 2: o // 2 + 2 * n].bitcast(F32)
        if len(shape) == 2:
            return v.rearrange("p (a b) -> p a b", a=shape[0])
        if len(shape) == 3:
            return v.rearrange("p (a b c) -> p a b c", a=shape[0], b=shape[1])
        return v

    o_xb = alloc(8 * TM * 4)
    XB = view(o_xb, F32, (8, TM))
    o_cst = alloc(NCST * 4)
    CST = view(o_cst, F32, (NCST,))
    o_mod = alloc(72 * 4)
    MOD = view(o_mod, F32, (72,))
    o_der = alloc(9 * 8 * 4)
    DER = view(o_der, F32, (9, 8))
    o_scb = alloc(8 * 2)
    SCB = view(o_scb, BF16, (8,))
    o_ones = alloc(128 * 2)
    ONES = view(o_ones, BF16, (128,))
    o_bones = alloc(128 * 2)
    BONES = view(o_bones, BF16, (128,))
    o_vones = alloc(2 * 64 * 2)
    VONES = view(o_vones, BF16, (2, 64))
    o_epsc = alloc(32)
    EPSC = view(o_epsc, F32, (8,))
    o_wpg = alloc(4 * 128 * 2)
    WPG = view(o_wpg, BF16, (4, 128))
    o_w = alloc(10 * 4096)
    WP = [Res() for _ in range(10)]

    def wpiece(i, shape, npieces=1):
        return view(o_w + 4096 * i, BF16, shape)

    o_uni = alloc(0)
    uni_base = off[0]

    off[0] = uni_base
    o_halo = alloc(8 * TB * 4)
    HALO = view(o_halo, F32, (8, TB))
    o_h = alloc(8 * (TM + TB) * 2)
    H = view(o_h, BF16, (8, TM + TB))
    o_actb = alloc(4 * (TM + TB) * 2)
    ACTB = view(o_actb, BF16, (4, TM + TB))
    RSTD = [view(alloc(TB * 4), F32, (TB,)) for _ in range(2)]
    o_tmp = [alloc(TB * 4) for _ in range(2)]
    TMP = [view(o, F32, (TB,)) for o in o_tmp]
    o_sa = [alloc(TB * 2) for _ in range(2)]
    SA = [view(o, BF16, (TB,)) for o in o_sa]
    o_sqf = alloc(8 * TB * 2)
    SQ_F = view(o_sqf, BF16, (8, TB))
    ACTB4 = view(o_sqf, BF16, (TM + TB,))
    WOX = [view(o_sqf + 5120, BF16, (1024,)), view(alloc(2048), BF16, (1024,))]
    ffn_end = off[0]

    off[0] = uni_base
    o_halo2 = alloc(8 * TB * 4)
    assert o_halo2 == o_halo
    KT = view(o_halo, BF16, (4, 1536))
    MRG = view(o_halo, BF16, (8, 1024))
    o_h2 = alloc(8 * 1536 * 2)
    H2 = view(o_h2, BF16, (8, 1536))
    o_v = alloc(12 * 512 * 2)
    V = view(o_v, BF16, (12, 512))
    o_expb = alloc(8 * TABW * 2)
    EXPB = view(o_expb, BF16, (8, TABW))
    o_reg = alloc(30976)
    TABT = view(o_reg, F32, (8 * TABW,))
    UB = [view(o_reg + 4160 * i, F32, (1040,)) for i in range(2)]
    TT = [view(o_reg + 8320 + 4160 * i, F32, (1040,)) for i in range(2)]
    MIXED = view(o_reg + 16640, BF16, (4, 1024))
    SQK = [view(o_reg + 24832 + 1024 * i, BF16, (TB,)) for i in range(2)]
    RK = [view(o_reg + 26880 + 2048 * i, F32, (TB,)) for i in range(2)]
    ATT = view(o_reg, BF16, (4, 1024))
    PRAW = [view(o_reg + 8192 + 3072 * i, BF16, (1536,)) for i in range(2)]
    PP = [view(o_reg + 14336 + 3072 * i, BF16, (1536,)) for i in range(3)]
    RDEN = [view(o_reg + 23552 + 1024 * i, F32, (256,)) for i in range(2)]
    SG = [view(o_reg + 8192 + 2048 * i, F32, (TB,)) for i in range(4)]
    M1 = [view(o_reg + 16384 + 2048 * i, F32, (TB,)) for i in range(2)]
    QT = view(o_w + 4096 * 6, BF16, (4, 1024))
    YP = view(o_w + 4096 * 8, BF16, (4, 1024))
    SQ_M = view(o_w + 4096 * 6, BF16, (8, TB))
    RSTD_M = [view(o_w + 4096 * 8 + 2048 * i, F32, (TB,)) for i in range(2)]
    TMP_M = [view(o_w + 4096 * 9 + 2048 * i, F32, (TB,)) for i in range(2)]
    mix_end = off[0]
    assert max(ffn_end, mix_end) <= ARENA_BYTES

    BANK = [Res() for _ in range(8)]
    VIEWS = dict(XB=XB, HALO=HALO, H=H, MOD=MOD, DER=DER, CST=CST, ACTB=ACTB, H2=H2, KT=KT, V=V, QT=QT, YP=YP,
                 EXPB=EXPB, ATT=ATT, MRG=MRG, MIXED=MIXED, SCB=SCB)

    def check(name):
        if stop == name:
            S.frozen = True

    def bank(i, n=1):
        return ps[:, i * 512:(i + n) * 512]

    def xblk(b, kt=None):
        if b == 0:
            return HALO[:, :, :] if kt is None else HALO[:, kt, :]
        if kt is None:
            return XB[:, :, (b - 1) * TB:b * TB]
        return XB[:, kt, (b - 1) * TB:b * TB]

    XR = [Res() for _ in range(5)]
    HR = [Res() for _ in range(5)]
    CSTR = Res()
    MODR = Res()
    DERR = Res()
    MISC = Res()

    S.op("sp", lambda e: e.dma_start(out=CST[:, :], in_=cst_d), writes=[CSTR], dma="d_cst")
    xv = xT.rearrange("(k p) t -> p k t", p=128)
    S.op("sp", lambda e: e.dma_start(out=xblk(0), in_=xv[:, :, 0:TB]), writes=[XR[0]], dma="d_x0")

    S.op("dve", lambda e: e.memset(ONES[:, :], 1.0 / 1024.0), writes=[MISC])
    S.op("dve", lambda e: e.memset(BONES[:, :], 0.0), writes=[MISC])
    S.op("dve", lambda e: e.memset(BONES[0:64, 0:64], 1.0 / 64.0), writes=[MISC])
    S.op("dve", lambda e: e.memset(BONES[64:128, 64:128], 1.0 / 64.0), writes=[MISC])
    S.op("dve", lambda e: e.memset(VONES[:, :, :], 1.0), writes=[MISC])
    S.op("dve", lambda e: e.memset(EPSC[:, :], EPS), writes=[MISC])
    S.op("dve", lambda e: e.tensor_scalar(out=VONES[:, 0, :], in0=VONES[:, 0, :], scalar1=CST[:, 110:111],
                                           scalar2=None, op0=ALU.mult), reads=[CSTR, MISC], writes=[MISC])
    S.op("act", lambda e: e.activation(out=SCB[:, :], in_=CST[:, 0:8], func=AF.Silu), reads=[CSTR], writes=[MISC])
    S.op("pool", lambda e: e.dma_start(out=WPG[:, :, :], in_=w_pg.rearrange("(g c) d -> c g d", c=128)),
         writes=[MISC], dma="d_wpg", nobar=True)

    ada_v = w_ada.rearrange("(k p) n -> p k n", p=128)
    MODPS = [bank(7), bank(7)]
    MODBANK = [BANK[7], BANK[7]]

    ADA_DMA = {}

    def ada_chunk(c):
        piece = 4 + (c % 2)
        wt = wpiece(piece, (8, 256))
        ADA_DMA[c] = S.op("pool", lambda e: e.dma_start(out=wt, in_=ada_v[:, :, c * 256:(c + 1) * 256]),
                          writes=[WP[piece]], dma="d_w%d" % piece, nobar=True)
        part = 0 if c < 12 else 1

        def mm(e):
            ins = None
            for jj in range(2):
                j = 2 * c + jj
                for kt in range(8):
                    ins = e.matmul(MODPS[part][:, j:j + 1], lhsT=wt[:, kt, jj * 128:(jj + 1) * 128],
                                   rhs=SCB[:, kt:kt + 1], start=(kt == 0), stop=(kt == 7))
            return ins
        S.op("pe", mm, reads=[WP[piece], MISC], writes=[MODBANK[part]], nobar=True)

    def mod_finish(part, lo, hi, rows):
        S.op("dve", lambda e: e.tensor_tensor(out=MOD[:, lo:hi], in0=MODPS[part][:, lo:hi], in1=CST[:, 8 + lo:8 + hi],
                                               op=ALU.add), reads=[MODBANK[part], CSTR], writes=[MODR], nobar=True)
        for r in rows:
            i, kind = r // 3, r % 3
            if kind == 0:
                gcol = 80 + 8 * i
                S.op("dve", lambda e, i=i, gcol=gcol: e.scalar_tensor_tensor(
                    out=DER[:, 3 * i, :], in0=MOD[:, 24 * i + 8:24 * i + 16], scalar=1.0, in1=CST[:, gcol:gcol + 8],
                    op0=ALU.add, op1=ALU.mult), reads=[MODR, CSTR], writes=[DERR], nobar=True)
            elif kind == 1:
                S.op("dve", lambda e, i=i: e.tensor_copy(out=DER[:, 3 * i + 1, :], in_=MOD[:, 24 * i:24 * i + 8]),
                     reads=[MODR], writes=[DERR], nobar=True)
            else:
                gs = (0.5, 1.0, 0.5)[i]
                S.op("dve", lambda e, i=i, gs=gs: e.tensor_scalar(
                    out=DER[:, 3 * i + 2, :], in0=MOD[:, 24 * i + 16:24 * i + 24], scalar1=gs, scalar2=None,
                    op0=ALU.mult), reads=[MODR], writes=[DERR], nobar=True)

    for c in range(8):
        ada_chunk(c)
    mod_finish(0, 0, 16, [0, 1])
    S.op("sp", lambda e: e.dma_start(out=xblk(1), in_=xv[:, :, TB:2 * TB]),
         writes=[XR[1]], dma="d_x1", deps=[ADA_DMA[5]])
    check("mod0")

    def norm_stats_a(xap_fn, xres, SQb, ssbank, part=None):
        xall = xap_fn(None)
        if part in (None, 0):
            S.op("act", lambda e: e.activation(out=SQb[:, :, :], in_=xall, func=AF.Square),
                 reads=[xres], writes=[NRES["sq"]])
        if part == 0:
            return

        def mm(e):
            ins = None
            for kt in range(8):
                ins = e.matmul(bank(ssbank), lhsT=ONES[:, :], rhs=SQb[:, kt, :], start=(kt == 0), stop=(kt == 7))
            return ins
        S.op("pe", mm, reads=[NRES["sq"], MISC], writes=[BANK[ssbank]])

    def norm_stats_b(RSTDb, rkey, ssbank):
        S.op("act", lambda e: e.activation(out=RSTDb[:, :], in_=bank(ssbank), func=AF.Ln, bias=EPSC[:, 0:1], scale=1.0),
             reads=[BANK[ssbank], MISC], writes=[NRES[rkey]])
        S.op("act", lambda e: e.activation(out=RSTDb[:, :], in_=RSTDb[:, :], func=AF.Exp, scale=-0.5),
             reads=[NRES[rkey]], writes=[NRES[rkey]])

    def norm_apply(xap_fn, xres, sub, out_fn, hres, RSTDb, rkey, TMPb):
        for kt in range(8):
            tb = TMPb[kt % 2]
            tr = NRES["tmp%d" % (kt % 2)]
            S.op("dve", lambda e, kt=kt, tb=tb: e.scalar_tensor_tensor(
                out=tb[:, :], in0=xap_fn(kt), scalar=DER[:, 3 * sub, kt:kt + 1], in1=RSTDb[:, :],
                op0=ALU.mult, op1=ALU.mult), reads=[xres, NRES[rkey], DERR], writes=[tr])
            if kt % 2 == 0:
                S.op("act", lambda e, kt=kt, tb=tb: e.activation(
                    out=out_fn(kt), in_=tb[:, :], func=AF.Identity, bias=DER[:, 3 * sub + 1, kt:kt + 1], scale=1.0),
                    reads=[tr, DERR], writes=[hres])
            else:
                S.op("dve", lambda e, kt=kt, tb=tb: e.tensor_scalar(
                    out=out_fn(kt), in0=tb[:, :], scalar1=DER[:, 3 * sub + 1, kt:kt + 1], scalar2=None, op0=ALU.add),
                    reads=[tr, DERR], writes=[hres])

    norm_ctr = [0]

    def norm_thunks(items, sub, SQb, RSTDs, TMPb, ssbank, all_dve=False):
        out = []
        for (xf, xr, of, hr) in items:
            i = norm_ctr[0]
            norm_ctr[0] += 1
            rb, rk = RSTDs[i % 2], "rstd%d" % (i % 2)
            th = [lambda xf=xf, xr=xr: norm_stats_a(xf, xr, SQb, ssbank, 0),
                  lambda xf=xf, xr=xr: norm_stats_a(xf, xr, SQb, ssbank, 1),
                  lambda rb=rb, rk=rk: norm_stats_b(rb, rk, ssbank)]
            for kt in range(8):
                th.append(lambda xf=xf, xr=xr, of=of, hr=hr, rb=rb, rk=rk, kt=kt:
                          norm_apply_kt(xf, xr, sub, of, hr, rb, rk, TMPb, kt, all_dve))
            out.append(th)
        return out

    def norm_apply_kt(xap_fn, xres, sub, out_fn, hres, RSTDb, rkey, TMPb, kt, all_dve=False):
        tb = TMPb[kt % 2]
        tr = NRES["tmp%d" % (kt % 2)]
        S.op("dve", lambda e: e.scalar_tensor_tensor(
            out=tb[:, :], in0=xap_fn(kt), scalar=DER[:, 3 * sub, kt:kt + 1], in1=RSTDb[:, :],
            op0=ALU.mult, op1=ALU.mult), reads=[xres, NRES[rkey], DERR], writes=[tr])
        if kt % 2 == 0 and not all_dve:
            S.op("act", lambda e: e.activation(
                out=out_fn(kt), in_=tb[:, :], func=AF.Identity, bias=DER[:, 3 * sub + 1, kt:kt + 1], scale=1.0),
                reads=[tr, DERR], writes=[hres])
        else:
            S.op("dve", lambda e: e.tensor_scalar(
                out=out_fn(kt), in0=tb[:, :], scalar1=DER[:, 3 * sub + 1, kt:kt + 1], scalar2=None, op0=ALU.add),
                reads=[tr, DERR], writes=[hres])

    def norm_seq(items, sub, SQb, RSTDs, TMPb, ssbank, extra=None, all_dve=False):
        ths = norm_thunks(items, sub, SQb, RSTDs, TMPb, ssbank, all_dve)
        extra = list(extra or [])
        n = len(ths)
        if n == 0:
            return
        ths[0][0]()
        ths[0][1]()
        ths[0][2]()
        for k in range(n):
            if k + 1 < n:
                ths[k + 1][0]()
                ths[k + 1][1]()
            for t in ths[k][3:]:
                t()
                if k >= 1 and extra:
                    extra.pop(0)()
            if k + 1 < n:
                ths[k + 1][2]()
        for t in extra:
            t()

    NRES = {k: Res() for k in ("sq", "rstd0", "rstd1", "tmp0", "tmp1")}

    outv = outT.rearrange("(k p) t -> p k t", p=128)

    FFN2_PRE = []

    def ffn(w_i, w_o_, sub, blocks, first):
        wi_v = w_i.rearrange("(k p) n -> p k n", p=128)
        wo_v = w_o_.rearrange("(k p) n -> p k n", p=128)
        pre = {}
        if first:
            wa0 = wpiece(0, (8, 256))
            wb0 = wpiece(1, (8, 256))
            S.op("pool", lambda e: e.dma_start(out=wa0, in_=wi_v[:, :, 0:256]),
                 writes=[WP[0]], dma="d_w0", nobar=True)
            S.op("pool", lambda e: e.dma_start(out=wb0, in_=wi_v[:, :, DFF:DFF + 256]),
                 writes=[WP[1]], dma="d_w1", nobar=True)
            pre[0] = True
            wlast = S.q["pool"][-1]
            for b in range(2, 5):
                S.op("sp", lambda e, b=b: e.dma_start(out=xblk(b), in_=xv[:, :, b * TB:(b + 1) * TB]),
                     writes=[XR[b]], dma="d_x%d" % b, deps=[wlast])

        def nitem(b):
            return (lambda kt, b=b: xblk(b, kt), XR[b], lambda kt, b=b: H[:, kt, b * TB:(b + 1) * TB], HR[b])
        pending_norms = list(blocks)
        if FFN2_PRE and not first:
            pending_norms.pop(0)
            th2 = norm_thunks([nitem(pending_norms.pop(0))], sub, SQ_F, RSTD, TMP, 6)[0]
            th2[0]()
            th2[1]()
            for t in FFN2_PRE[0]:
                t()
            th2[2]()
            for t in th2[3:]:
                t()
        else:
            norm_seq([nitem(pending_norms.pop(0)), nitem(pending_norms.pop(0))], sub, SQ_F, RSTD, TMP, 6)
        ACTR = {}
        SAR = [Res(), Res()]
        cnt = [0]
        ada_next = [8] if first else [36]
        GSIZES = [4, 4, 4, 5, 5]
        WOXR = [Res(), Res()]
        GSTART = [sum(GSIZES[:g]) for g in range(len(GSIZES))]
        ngroups = len(GSIZES)
        chunk_ctr = [0]

        def wi_load(g, c2):
            j0 = GSTART[g] + 2 * c2
            nt = min(2, GSTART[g] + GSIZES[g] - j0)
            sl = chunk_ctr[0] % 2
            chunk_ctr[0] += 1
            wa = wpiece(2 * sl, (8, 256))
            wb = wpiece(2 * sl + 1, (8, 256))
            if not (g == 0 and c2 == 0 and pre.get(0)):
                S.op("pool", lambda e: e.dma_start(out=wa[:, :, 0:nt * 128], in_=wi_v[:, :, j0 * 128:(j0 + nt) * 128]),
                     writes=[WP[2 * sl]], dma="d_w%d" % (2 * sl), nobar=True)
                S.op("pool", lambda e: e.dma_start(out=wb[:, :, 0:nt * 128],
                                                   in_=wi_v[:, :, DFF + j0 * 128:DFF + (j0 + nt) * 128]),
                     writes=[WP[2 * sl + 1]], dma="d_w%d" % (2 * sl + 1), nobar=True)
            return sl, wa, wb, nt

        def actb(jl, b):
            if jl == 4:
                return ACTB4[:, b * TB:(b + 1) * TB]
            return ACTB[:, jl, b * TB:(b + 1) * TB]

        def unit(g, c2, jj, b, sl, wa, wb):
            jl = 2 * c2 + jj
            i = cnt[0]
            cnt[0] += 1
            ba, bb = i % 2, 2 + i % 2
            hb = H[:, :, b * TB:(b + 1) * TB]

            def mma(e):
                ins = None
                for kt in range(8):
                    ins = e.matmul(bank(ba), lhsT=wa[:, kt, jj * 128:(jj + 1) * 128], rhs=hb[:, kt, :],
                                   start=(kt == 0), stop=(kt == 7))
                return ins

            def mmb(e):
                ins = None
                for kt in range(8):
                    ins = e.matmul(bank(bb), lhsT=wb[:, kt, jj * 128:(jj + 1) * 128], rhs=hb[:, kt, :],
                                   start=(kt == 0), stop=(kt == 7))
                return ins
            S.op("pe", mma, reads=[WP[2 * sl], HR[b]], writes=[BANK[ba]])
            S.op("pe", mmb, reads=[WP[2 * sl + 1], HR[b]], writes=[BANK[bb]])
            S.op("act", lambda e: e.activation(out=SA[i % 2][:, :], in_=bank(ba), func=AF.Silu),
                 reads=[BANK[ba]], writes=[SAR[i % 2]])
            ar = ACTR.setdefault((jl, b), Res())
            S.op("dve", lambda e: e.tensor_tensor(
                out=actb(jl, b), in0=SA[i % 2][:, :], in1=bank(bb), op=ALU.mult),
                reads=[SAR[i % 2], BANK[bb]], writes=[ar],
                deps=([NRES["sq"].w] + list(NRES["sq"].r.values())) if jl == 4 else ())
            if ada_next[0] < 36 and ((ada_next[0] < 12 and i >= 8) or (ada_next[0] >= 12 and i % 3 == 2)):
                ada_chunk(ada_next[0])
                ada_next[0] += 1
                if ada_next[0] == 12:
                    mod_finish(0, 16, 24, [2])
                if ada_next[0] == 36:
                    mod_finish(1, 24, 72, [3, 4, 5, 6, 7, 8])

        for g in range(ngroups):
            nk = GSIZES[g]
            nch = (nk + 1) // 2
            if g == 0:
                chunks = [wi_load(g, c2) for c2 in range(nch)]
                nth_ = []
                carry, carry_next = [], []
                for b in blocks:
                    u = 0
                    for t in carry:
                        t()
                    carry, carry_next = carry_next, []
                    for c2 in range(nch):
                        sl_, wa_, wb_, nt_ = chunks[c2]
                        for jj in range(nt_):
                            unit(g, c2, jj, b, sl_, wa_, wb_)
                            u += 1
                            if u == 3 and pending_norms:
                                nth_ = norm_thunks([nitem(pending_norms.pop(0))], sub, SQ_F, RSTD, TMP, 6, False)[0]
                                nth_[0]()
                            elif u == 4 and nth_:
                                nth_[1]()
                                nth_[2]()
                                carry_next = nth_[3:]
                                nth_ = []
                            for t in carry[:2]:
                                t()
                            carry = carry[2:]
                for t in carry + carry_next:
                    t()
            else:
                for c2 in range(nch):
                    sl_, wa_, wb_, nt_ = wi_load(g, c2)
                    for jj in range(nt_):
                        for b in blocks:
                            unit(g, c2, jj, b, sl_, wa_, wb_)
            so = g % 2
            wo = view(o_w + 4096 * (6 + 2 * so), BF16, (4, 1024))
            nk4 = min(nk, 4)
            S.op("pool", lambda e, wo=wo, g=g, nk4=nk4: e.dma_start(out=wo[:, 0:nk4, :],
                                                                    in_=wo_v[:, GSTART[g]:GSTART[g] + nk4, :]),
                 writes=[WP[6 + 2 * so], WP[7 + 2 * so]], dma="d_w%d" % (6 + 2 * so))
            wox = WOX[so]
            if nk == 5:
                S.op("pool", lambda e, wox=wox, g=g: e.dma_start(out=wox[:, :], in_=wo_v[:, GSTART[g] + 4, :]),
                     writes=[WOXR[so]], dma="d_wox%d" % so,
                     deps=[NRES["sq"].w] + list(NRES["sq"].r.values()))
            last = (g == ngroups - 1)
            order = ([(b, n) for b in blocks for n in range(0, 8, 2)] if last
                     else [(b, n) for n in range(0, 8, 2) for b in blocks])
            k = 0
            for (b, n) in order:
                ob = (4, 0, 2)[k % 3]
                k += 1

                def mmo(e, wo=wo, wox=wox, n=n, b=b, ob=ob, nk=nk):
                    ins = None
                    for dn in range(2):
                        cs = slice((n + dn) * 128, (n + dn + 1) * 128)
                        for jl in range(nk):
                            lw = wox[:, cs] if jl == 4 else wo[:, jl, cs]
                            ins = e.matmul(bank(ob + dn), lhsT=lw, rhs=actb(jl, b),
                                           start=(jl == 0), stop=(jl == nk - 1))
                    return ins
                S.op("pe", mmo, reads=[WP[6 + 2 * so], WP[7 + 2 * so]] + ([WOXR[so]] if nk == 5 else [])
                     + [ACTR[(jl, b)] for jl in range(nk)], writes=[BANK[ob], BANK[ob + 1]])
                evs = []
                for dn in range(2):
                    evs.append(S.op("dve", lambda e, n=n + dn, b=b, obb=ob + dn: e.scalar_tensor_tensor(
                        out=xblk(b, n), in0=bank(obb), scalar=DER[:, 3 * sub + 2, n:n + 1], in1=xblk(b, n),
                        op0=ALU.mult, op1=ALU.add), reads=[BANK[ob + dn], DERR], writes=[XR[b]]))
                if last and not first:
                    S.op("sp", lambda e, n=n, b=b: e.dma_start(
                        out=outv[:, n:n + 2, (b - 1) * TB:b * TB], in_=XB[:, n:n + 2, (b - 1) * TB:b * TB]),
                        deps=evs, dma="d_st", force=True)

    ffn(w1i, w1o, 0, [0, 1, 2, 3, 4], True)
    check("ffn1")
    S.barrier()

    TABR = Res()
    S.op("sp", lambda e: e.dma_start(out=TABT[:, :], in_=tab_d), writes=[TABR], dma="d_tab")
    EXPR = Res()

    win_v = w_in.rearrange("(k p) n -> p k n", p=128)
    ring = [0]
    RING = [0, 1, 2, 3, 4, 5]

    def load_w(src_ap, shape):
        piece = RING[ring[0] % 6]
        ring[0] += 1
        wt = wpiece(piece, shape)
        S.op("pool", lambda e: e.dma_start(out=wt, in_=src_ap), writes=[WP[piece]], dma="d_w%d" % piece, nobar=True)
        return wt, WP[piece]

    wao_v = w_ao.rearrange("(k p) n -> p k n", p=128)
    wpo_v = w_po.rearrange("(k p) n -> p k n", p=128)
    wo_v2 = w_o.rearrange("(k p) n -> p k n", p=128)
    expb_pstride = EXPB[:, 0, :].ap[0][0]

    for pas in (0, 1):
        eblk = [2 * pas + e for e in range(3)]
        if pas == 0:
            items = []
            for e_i, b in enumerate(eblk):
                items.append((lambda kt, b=b: xblk(b, kt), XR[b],
                              lambda kt, e_i=e_i: H2[:, kt, e_i * TB:(e_i + 1) * TB], HR[e_i]))
            tab_th = [lambda h=h: S.op("act", lambda e: e.activation(
                out=EXPB[:, h, :], in_=TABT[:, h * TABW:(h + 1) * TABW], func=AF.Exp),
                reads=[TABR], writes=[EXPR]) for h in range(8)]
            norm_seq(items, 1, SQ_M, RSTD_M, TMP_M, 6, extra=tab_th)
        check("s1_%d" % pas)
        S.barrier()
        KR = [Res() for _ in range(3)]
        QR = Res()
        VR = [Res() for _ in range(12)]
        SQKR = [Res(), Res()]
        RKR = [Res(), Res()]
        cnt = 0
        UR = [Res(), Res()]
        TR = [Res(), Res()]
        MXR = [Res() for _ in range(4)]
        YPR = Res()
        uw = [load_w(win_v[:, :, 1536 + c * 256:1536 + (c + 1) * 256], (8, 256)) for c in range(2)]

        def u_proj(gq):
            wt, wr = uw[gq // 2]
            f = gq % 2
            ub = UB[gq % 2]
            ur = UR[gq % 2]
            for e_i in range(3):
                ubk = 6 + (3 * gq + e_i) % 2
                if e_i == 0:
                    rhs_fn = lambda kt: H2[:, kt, TB - 64:TB]
                    ncol = 64
                else:
                    rhs_fn = lambda kt, e_i=e_i: H2[:, kt, e_i * TB:(e_i + 1) * TB]
                    ncol = TB

                def mmu(e, rhs_fn=rhs_fn, ubk=ubk, ncol=ncol):
                    ins = None
                    for kt in range(8):
                        ins = e.matmul(bank(ubk)[:, 0:ncol], lhsT=wt[:, kt, f * 128:(f + 1) * 128], rhs=rhs_fn(kt),
                                       start=(kt == 0), stop=(kt == 7))
                    return ins
                S.op("pe", mmu, reads=[wr, HR[e_i]], writes=[BANK[ubk]])
                if e_i == 0:
                    if pas == 0:
                        S.op("dve", lambda e, ubk=ubk: e.tensor_scalar(
                            out=ub[:, 0:16], in0=bank(ubk)[:, 48:64], scalar1=CST[:, 110:111], scalar2=None,
                            op0=ALU.mult), reads=[BANK[ubk], CSTR], writes=[ur])
                    else:
                        S.op("dve", lambda e, ubk=ubk: e.tensor_copy(out=ub[:, 0:16], in_=bank(ubk)[:, 48:64]),
                             reads=[BANK[ubk]], writes=[ur])
                else:
                    S.op("act", lambda e, ubk=ubk, e_i=e_i: e.activation(
                        out=ub[:, 16 + (e_i - 1) * TB:16 + e_i * TB], in_=bank(ubk), func=AF.Copy),
                        reads=[BANK[ubk]], writes=[ur])

        def u_pool(gq):
            ub = UB[gq % 2]
            ur = UR[gq % 2]
            th = []
            nsteps = gq + 1
            src, sres = ub, ur
            for st in range(nsteps):
                sh = 1 << st
                dst, dres = TT[st % 2], TR[st % 2]
                lo = 2 * sh - 1
                th.append(lambda src=src, dst=dst, sh=sh, lo=lo, sres=sres, dres=dres: S.op(
                    "dve", lambda e: e.tensor_tensor(
                        out=dst[:, lo:1040], in0=src[:, lo:1040], in1=src[:, lo - sh:1040 - sh], op=ALU.add),
                    reads=[sres], writes=[dres]))
                src, sres = dst, dres
            w = 2 << gq
            th.append(lambda src=src, sres=sres: S.op("dve", lambda e: e.scalar_tensor_tensor(
                out=MIXED[:, gq, :], in0=src[:, 16:1040], scalar=1.0 / w, in1=ub[:, 16:1040],
                op0=ALU.mult, op1=ALU.subtract), reads=[sres, ur], writes=[MXR[gq]]))
            if pas == 0:
                t16 = TT[(nsteps) % 2]
                t16r = TR[(nsteps) % 2]
                th.append(lambda src=src, sres=sres: S.op("dve", lambda e: e.tensor_tensor(
                    out=t16[:, 0:16], in0=src[:, 16:32], in1=CST[:, 112 + 16 * gq:128 + 16 * gq], op=ALU.mult),
                    reads=[sres, CSTR], writes=[t16r]))
                th.append(lambda: S.op("dve", lambda e: e.tensor_tensor(
                    out=MIXED[:, gq, 0:16], in0=t16[:, 0:16], in1=ub[:, 16:32], op=ALU.subtract),
                    reads=[t16r, ur], writes=[MXR[gq]]))
            return th

        def u_yp(gq):
            for mb in range(2):
                ypb = 4 + mb
                S.op("pe", lambda e, mb=mb, ypb=ypb: e.matmul(
                    bank(ypb), lhsT=WPG[:, gq, :], rhs=MIXED[:, gq, mb * TB:(mb + 1) * TB], start=True, stop=True),
                    reads=[MXR[gq], MISC], writes=[BANK[ypb]])
                S.op("dve", lambda e, mb=mb, ypb=ypb: e.tensor_scalar(
                    out=YP[:, gq, mb * TB:(mb + 1) * TB], in0=bank(ypb), scalar1=CST[:, 106 + gq:107 + gq],
                    scalar2=None, op0=ALU.mult), reads=[BANK[ypb], CSTR], writes=[YPR])

        u_proj(0)
        u_proj(1)
        pool_q = u_pool(0) + u_pool(1)
        units = []
        for which, col0, eb_list, gcol in (("k", 512, [0, 1, 2], 105), ("q", 0, [1, 2], 104)):
            for c in range(2):
                wt, wr = load_w(win_v[:, :, col0 + c * 256:col0 + (c + 1) * 256], (8, 256))
                for f in range(2):
                    for e_i in eb_list:
                        units.append((which, wt, wr, f, 2 * c + f, e_i, gcol))

        def kq_raw(i):
            which, wt, wr, f, ft, e_i, gcol = units[i]
            pb = i % 4
            hb = H2[:, :, e_i * TB:(e_i + 1) * TB]

            def mmk(e):
                ins = None
                for kt in range(8):
                    ins = e.matmul(bank(pb), lhsT=wt[:, kt, f * 128:(f + 1) * 128], rhs=hb[:, kt, :],
                                   start=(kt == 0), stop=(kt == 7))
                return ins
            S.op("pe", mmk, reads=[wr, HR[e_i]], writes=[BANK[pb]])
            S.op("act", lambda e: e.activation(out=SQK[i % 2][:, :], in_=bank(pb), func=AF.Square),
                 reads=[BANK[pb]], writes=[SQKR[i % 2]])

        def kq_rest(i):
            which, wt, wr, f, ft, e_i, gcol = units[i]
            pb = i % 4
            sb_ = 4 + i % 2
            S.op("pe", lambda e: e.matmul(bank(sb_), lhsT=BONES[:, :], rhs=SQK[i % 2][:, :], start=True, stop=True),
                 reads=[SQKR[i % 2], MISC], writes=[BANK[sb_]])
            S.op("act", lambda e: e.activation(out=RK[i % 2][:, :], in_=bank(sb_), func=AF.Ln, bias=EPSC[:, 0:1],
                                               scale=1.0), reads=[BANK[sb_], MISC], writes=[RKR[i % 2]])
            S.op("act", lambda e: e.activation(out=RK[i % 2][:, :], in_=RK[i % 2][:, :], func=AF.Exp, scale=-0.5),
                 reads=[RKR[i % 2]], writes=[RKR[i % 2]])
            if which == "k":
                dst = KT[:, ft, e_i * TB:(e_i + 1) * TB]
                dres = KR[e_i]
            else:
                dst = QT[:, ft, (e_i - 1) * TB:e_i * TB]
                dres = QR
            S.op("dve", lambda e: e.scalar_tensor_tensor(
                out=dst, in0=bank(pb), scalar=CST[:, gcol:gcol + 1], in1=RK[i % 2][:, :],
                op0=ALU.mult, op1=ALU.mult), reads=[BANK[pb], RKR[i % 2], CSTR], writes=[dres])

        kq_raw(0)
        for i in range(len(units)):
            if i + 1 < len(units):
                kq_raw(i + 1)
            kq_rest(i)
            if pool_q:
                pool_q.pop(0)()
            if i == 9:
                while pool_q:
                    pool_q.pop(0)()
                u_proj(2)
                u_proj(3)
                pool_q = u_pool(2) + u_pool(3)
        while pool_q:
            pool_q.pop(0)()
        wv0, wvr0 = load_w(win_v[:, :, 1024:1280], (8, 256))
        wv1, wvr1 = load_w(win_v[:, :, 1280:1536], (8, 256))
        for t in range(12):
            vb = 4 + t % 2

            def mmv(e, t=t, vb=vb, wv0=wv0, wv1=wv1):
                ins = None
                for half, wv in ((0, wv0), (1, wv1)):
                    for kt in range(8):
                        ins = e.matmul(bank(vb)[:, half * 256:(half + 1) * 256], lhsT=H2[:, kt, t * 128:(t + 1) * 128],
                                       rhs=wv[:, kt, :], start=(kt == 0), stop=(kt == 7))
                return ins
            S.op("pe", mmv, reads=[wvr0, wvr1, HR[t // 4]], writes=[BANK[vb]])
            if pas == 0 and t < 4:
                S.op("dve", lambda e, t=t, vb=vb: e.tensor_scalar(out=V[:, t, :], in0=bank(vb), scalar1=CST[:, 110:111],
                                                                  scalar2=None, op0=ALU.mult),
                     reads=[BANK[vb], CSTR], writes=[VR[t]])
            else:
                S.op("act", lambda e, t=t, vb=vb: e.activation(out=V[:, t, :], in_=bank(vb), func=AF.Copy),
                     reads=[BANK[vb]], writes=[VR[t]])
        for gq in range(4):
            u_yp(gq)
        check("s2_%d" % pas)
        S.barrier()
        PRR = [Res(), Res()]
        PPR = [Res(), Res(), Res()]
        RDR = [Res(), Res()]
        ATTR = Res()
        def att_front(u):
            qb, h = u // 8, u % 8
            s = u % 2
            ft, po = h // 2, (h % 2) * 64
            sb0 = 3 * s

            def mms(e):
                ins = None
                for jj in range(6):
                    kt_ = 2 * qb + 5 - jj
                    ins = e.matmul(ps[:, sb0 * 512 + jj * 256:sb0 * 512 + (jj + 1) * 256],
                                   lhsT=KT[po:po + 64, ft, kt_ * 128:(kt_ + 1) * 128],
                                   rhs=QT[po:po + 64, ft, qb * 256:(qb + 1) * 256], start=True, stop=True)
                return ins
            S.op("pe", mms, reads=KR + [QR], writes=[BANK[sb0], BANK[sb0 + 1], BANK[sb0 + 2]])
            S.op("act", lambda e: e.activation(out=PRAW[s][:, :], in_=bank(sb0, 3), func=AF.Exp, scale=0.125),
                 reads=[BANK[sb0], BANK[sb0 + 1], BANK[sb0 + 2]], writes=[PRR[s]])
            ewin = bass.AP(EXPB.tensor, EXPB[:, h, 0:1].offset, [[expb_pstride, 128], [128, 6], [1, 256]])
            S.op("dve", lambda e: e.tensor_tensor(
                out=PP[u % 3][:, :].rearrange("p (a b) -> p a b", a=6),
                in0=PRAW[s][:, :].rearrange("p (a b) -> p a b", a=6), in1=ewin, op=ALU.mult),
                reads=[PRR[s], EXPR], writes=[PPR[u % 3]])

        def att_back(u, pas=pas):
            qb, h = u // 8, u % 8
            s = u % 2
            ft, po = h // 2, (h % 2) * 64
            pr = (u // 2) % 2
            ndb = 6 + pr

            def mmpv(e):
                ins = None
                for jj in range(6):
                    kt_ = 2 * qb + 5 - jj
                    ins = e.matmul(bank(ndb)[po:po + 64, 0:256], lhsT=V[:, kt_, h * 64:(h + 1) * 64],
                                   rhs=PP[u % 3][:, jj * 256:(jj + 1) * 256], start=(jj == 0), stop=(jj == 5))
                for jj in range(6):
                    kt_ = 2 * qb + 5 - jj
                    var = 0 if (pas == 0 and kt_ < 4) else 1
                    ins = e.matmul(bank(ndb)[po:po + 64, 256:512], lhsT=VONES[:, var, :],
                                   rhs=PP[u % 3][:, jj * 256:(jj + 1) * 256], start=(jj == 0), stop=(jj == 5))
                return ins
            S.op("pe", mmpv, reads=VR + [PPR[u % 3], MISC], writes=[BANK[ndb]])
            if h % 2 == 1:
                S.op("act", lambda e: e.activation(out=RDEN[pr][:, :], in_=bank(ndb)[:, 256:512], func=AF.Ln),
                     reads=[BANK[ndb]], writes=[RDR[pr]])
                S.op("act", lambda e: e.activation(out=RDEN[pr][:, :], in_=RDEN[pr][:, :], func=AF.Exp, scale=-1.0),
                     reads=[RDR[pr]], writes=[RDR[pr]])
                S.op("dve", lambda e: e.tensor_tensor(
                    out=ATT[:, ft, qb * 256:(qb + 1) * 256], in0=bank(ndb)[:, 0:256],
                    in1=RDEN[pr][:, :], op=ALU.mult), reads=[BANK[ndb], RDR[pr]], writes=[ATTR])

        att_front(0)
        att_front(1)
        for u in range(32):
            if u + 2 < 32:
                att_front(u + 2)
            att_back(u)
        check("s3_%d" % pas)
        S.barrier()
        SGR = [Res() for _ in range(4)]
        M1R = [Res(), Res()]
        MRGR = Res()
        it = 0
        for c in range(4):
            wga, rga = load_w(win_v[:, :, 2048 + c * 256:2048 + (c + 1) * 256], (8, 256))
            wgb, rgb = load_w(win_v[:, :, 3072 + c * 256:3072 + (c + 1) * 256], (8, 256))
            wao, rao = load_w(wao_v[:, :, c * 256:(c + 1) * 256], (4, 256))
            wpo, rpo = load_w(wpo_v[:, :, c * 256:(c + 1) * 256], (4, 256))
            for f in range(2):
                n = 2 * c + f
                for mb in range(2):
                    s = it % 2
                    it += 1
                    b0 = 4 * s
                    hb = H2[:, :, (1 + mb) * TB:(2 + mb) * TB]
                    fs = slice(f * 128, (f + 1) * 128)
                    ms = slice(mb * TB, (mb + 1) * TB)

                    def mm_ya(e, wao=wao, fs=fs, ms=ms, b0=b0):
                        ins = None
                        for hp in range(4):
                            ins = e.matmul(bank(b0), lhsT=wao[:, hp, fs], rhs=ATT[:, hp, ms], start=(hp == 0), stop=(hp == 3))
                        return ins

                    def mm_yb(e, wpo=wpo, fs=fs, ms=ms, b0=b0):
                        ins = None
                        for g in range(4):
                            ins = e.matmul(bank(b0 + 1), lhsT=wpo[:, g, fs], rhs=YP[:, g, ms], start=(g == 0), stop=(g == 3))
                        return ins

                    def mm_ga(e, wga=wga, fs=fs, hb=hb, b0=b0):
                        ins = None
                        for kt in range(8):
                            ins = e.matmul(bank(b0 + 2), lhsT=wga[:, kt, fs], rhs=hb[:, kt, :], start=(kt == 0), stop=(kt == 7))
                        return ins

                    def mm_gb(e, wgb=wgb, fs=fs, hb=hb, b0=b0):
                        ins = None
                        for kt in range(8):
                            ins = e.matmul(bank(b0 + 3), lhsT=wgb[:, kt, fs], rhs=hb[:, kt, :], start=(kt == 0), stop=(kt == 7))
                        return ins
                    S.op("pe", mm_ga, reads=[rga, HR[1 + mb]], writes=[BANK[b0 + 2]])
                    S.op("pe", mm_gb, reads=[rgb, HR[1 + mb]], writes=[BANK[b0 + 3]])
                    S.op("pe", mm_ya, reads=[rao, ATTR], writes=[BANK[b0]])
                    S.op("pe", mm_yb, reads=[rpo, YPR], writes=[BANK[b0 + 1]])
                    S.op("act", lambda e, s=s, b0=b0: e.activation(out=SG[2 * s][:, :], in_=bank(b0 + 2), func=AF.Sigmoid),
                         reads=[BANK[b0 + 2]], writes=[SGR[2 * s]])
                    S.op("act", lambda e, s=s, b0=b0: e.activation(out=SG[2 * s + 1][:, :], in_=bank(b0 + 3), func=AF.Sigmoid),
                         reads=[BANK[b0 + 3]], writes=[SGR[2 * s + 1]])
                    S.op("dve", lambda e, s=s, b0=b0: e.tensor_tensor(out=M1[s][:, :], in0=bank(b0), in1=SG[2 * s][:, :],
                                                                      op=ALU.mult),
                         reads=[BANK[b0], SGR[2 * s]], writes=[M1R[s]])
                    S.op("dve", lambda e, s=s, b0=b0: e.tensor_tensor(out=SG[2 * s + 1][:, :], in0=bank(b0 + 1),
                                                                      in1=SG[2 * s + 1][:, :], op=ALU.mult),
                         reads=[BANK[b0 + 1], SGR[2 * s + 1]], writes=[SGR[2 * s + 1]])
                    S.op("dve", lambda e, s=s, n=n, ms=ms: e.tensor_tensor(out=MRG[:, n, ms], in0=M1[s][:, :],
                                                                           in1=SG[2 * s + 1][:, :], op=ALU.add),
                         reads=[M1R[s], SGR[2 * s + 1]], writes=[MRGR])
        check("s4_%d" % pas)
        S.barrier()
        side = []
        if pas == 0:
            side.append(lambda: S.op("dve", lambda e: e.tensor_copy(out=H2[:, :, 0:TB], in_=H2[:, :, 2 * TB:3 * TB]),
                                     reads=[HR[2]], writes=[HR[0]]))
            its = [(lambda kt, b=b: xblk(b, kt), XR[b],
                    lambda kt, e_i=e_i: H2[:, kt, e_i * TB:(e_i + 1) * TB], HR[e_i]) for e_i, b in ((1, 3), (2, 4))]
            nt = norm_thunks(its, 1, SQ_M, RSTD_M, TMP_M, 6)
            side += nt[0][:3] + nt[1][:2] + nt[0][3:] + [nt[1][2]] + nt[1][3:]
        else:
            pre_nt = norm_thunks([(lambda kt: xblk(1, kt), XR[1], lambda kt: H[:, kt, TB:2 * TB], HR[1])],
                                 2, SQ_F, RSTD, TMP, 6)[0]
            side += pre_nt[:3]
            FFN2_PRE.append(pre_nt[3:])
        it = 0
        for c in range(4):
            wo_, ro_ = load_w(wo_v2[:, :, c * 256:(c + 1) * 256], (8, 256))
            for f in range(2):
                n = 2 * c + f
                for mb in range(2):
                    ob = it % 2
                    it += 1
                    b = 2 * pas + 1 + mb
                    ms = slice(mb * TB, (mb + 1) * TB)

                    def mm_o(e, wo_=wo_, f=f, ms=ms, ob=ob):
                        ins = None
                        for kt in range(8):
                            ins = e.matmul(bank(ob), lhsT=wo_[:, kt, f * 128:(f + 1) * 128], rhs=MRG[:, kt, ms],
                                           start=(kt == 0), stop=(kt == 7))
                        return ins
                    S.op("pe", mm_o, reads=[ro_, MRGR], writes=[BANK[ob]])
                    S.op("dve", lambda e, n=n, b=b, ob=ob: e.scalar_tensor_tensor(
                        out=xblk(b, n), in0=bank(ob), scalar=DER[:, 5, n:n + 1], in1=xblk(b, n),
                        op0=ALU.mult, op1=ALU.add), reads=[BANK[ob], DERR], writes=[XR[b]])
                    for t in side[:2]:
                        t()
                    side = side[2:]
        for t in side:
            t()
        check("s5_%d" % pas)
        S.barrier()

    ffn(w3i, w3o, 2, [1, 2, 3, 4], False)
    STR = Res()
    if S.frozen:
        for b in range(1, 5):
            S.op("sp", lambda e, b=b: e.dma_start(out=outv[:, :, (b - 1) * TB:b * TB], in_=xblk(b)),
                 reads=[XR[b]], writes=[STR], dma="d_st", force=True)
    if dbg_d is not None:
        S.op("pool", lambda e: e.dma_start(out=(dbg_d if len(debug[0](VIEWS).shape) == 2 else dbg_d.rearrange("p (a b) -> p a b", a=debug[0](VIEWS).shape[1])), in_=debug[0](VIEWS)), reads=[], writes=[STR], dma="d_dbg",
             deps=[S.q[x][-1] for x in ("pe", "act", "dve") if S.q[x]], force=True)
        dbg_op = S.q["pool"][-1]
    final = S.q["sp"][-1]

    keys = S.finalize()
    total_st = final.count
    sem_cms = {k: nc.semaphore(k) for k in keys}
    sems = {k: cm.__enter__() for k, cm in sem_cms.items()}
    with nc.Block() as block:
        @block.sync
        def _(e):
            S.emit("sp", e, sems)
            e.wait_ge(sems["d_st"], total_st)
            if dbg_d is not None:
                e.wait_ge(sems["d_dbg"], 16)

        @block.gpsimd
        def _(e):
            S.emit("pool", e, sems)

        @block.tensor
        def _(e):
            S.emit("pe", e, sems)

        @block.scalar
        def _(e):
            S.emit("act", e, sems)

        @block.vector
        def _(e):
            S.emit("dve", e, sems)
    for cm in sem_cms.values():
        cm.__exit__(None, None, None)
    psum_cm.__exit__(None, None, None)
    arena_cm.__exit__(None, None, None)
    return nc


def _prep(inp):
    f = lambda k: np.asarray(inp[k], dtype=np.float32)
    x = f("x")
    c = f("c")
    rel_bias = f("rel_bias")[0]
    k = np.arange(128)[:, None]
    i = np.arange(TABW)[None, :]
    d = i - 128 - k
    cd = i // 64 - 2 - k // 64
    valid = (cd >= 0) & (cd <= 8)
    idx = np.clip(d, -128, 128) + 128
    tab = np.empty((128, 8, TABW), np.float32)
    for h in range(8):
        tab[:, h, :] = np.where(valid, rel_bias[h][idx], np.float32(-30000.0))
    tab = np.ascontiguousarray(tab.reshape(128, 8 * TABW))
    shared = {
        "w_ada": np.ascontiguousarray(f("w_ada")[0]),
        "w1i": np.ascontiguousarray(f("w_ffn1_in")[0]),
        "w1o": np.ascontiguousarray(f("w_ffn1_out")[0]),
        "w3i": np.ascontiguousarray(f("w_ffn2_in")[0]),
        "w3o": np.ascontiguousarray(f("w_ffn2_out")[0]),
        "w_in": np.ascontiguousarray(f("w_in")[0]),
        "tab": tab,
        "w_ao": np.ascontiguousarray(f("w_attn_out")[0]),
        "w_pg": np.ascontiguousarray(f("w_pool_group")[0].reshape(512, 128)),
        "w_po": np.ascontiguousarray(f("w_pool_out")[0]),
        "w_o": np.ascontiguousarray(f("w_o")[0]),
    }
    col = lambda v: v.reshape(-1, 128).T
    in_maps = []
    wins = (2, 4, 8, 16)
    for core in range(8):
        b, s = core // 4, core % 4
        xe = np.zeros((TM + TB, D), np.float32)
        if s == 0:
            xe[TB:] = x[b, 0:TM]
        else:
            xe[:] = x[b, s * TM - TB:(s + 1) * TM]
        cst = np.zeros((128, NCST), np.float32)
        cst[:, 0:8] = col(c[b])
        cst[:, 8:80] = col(f("b_ada")[0])
        cst[:, 80:88] = col(f("g_ffn1")[0])
        cst[:, 88:96] = col(f("g_mix")[0])
        cst[:, 96:104] = col(f("g_ffn2")[0])
        cst[:, 104] = np.tile(f("q_gain")[0], 2)
        cst[:, 105] = np.tile(f("k_gain")[0], 2)
        cst[:, 106:110] = col(f("pool_scale")[0])
        cst[:, 110] = 0.0 if s == 0 else 1.0
        cst[:, 111] = 1.0
        for g in range(4):
            for t in range(16):
                cnt = min(t + 1, wins[g]) if s == 0 else wins[g]
                cst[:, 112 + 16 * g + t] = np.float32(1.0) / np.float32(cnt)
        m = dict(shared)
        m["xT"] = np.ascontiguousarray(xe.T)
        m["cst"] = cst
        in_maps.append(m)
    return in_maps


_NC_CACHE = {}


def kernel(**inp):
    in_maps = _prep(inp)
    if "nc" not in _NC_CACHE:
        _NC_CACHE["nc"] = build_nc()
    nc = _NC_CACHE["nc"]
    res = run_bass_kernel_spmd(nc, in_maps, core_ids=list(range(8)))
    out = np.empty((2, 8192, D), np.float32)
    for core in range(8):
        b, s = core // 4, core % 4
        out[b, s * TM:(s + 1) * TM, :] = np.asarray(res.results[core]["outT"]).T
    return out
```
